# Optimizing a Trainium2 kernel written in Bass

```python
import jax, jax.numpy as jnp
from jax import lax
import numpy as np

D_MODEL = 1024
BATCH = 8
SEQ = 2048
DEPTH = 1
DEC_BATCH = 128
DEC_SEQ = 8
PAST_LEN = 8192
PAGE_SIZE = 128

D_CONV = D_MODEL
CONV_WIDTH = 3
HEAD_DIM = 64
N_HEADS = D_MODEL // HEAD_DIM
N_KV_HEADS = 4
GROUP = N_HEADS // N_KV_HEADS
WINDOW = 128
BLOCK = WINDOW
D_FF = 2816
Q_WIDTH = N_HEADS * HEAD_DIM
KV_WIDTH = N_KV_HEADS * HEAD_DIM
IN_WIDTH = 3 * D_CONV + Q_WIDTH + 2 * KV_WIDTH + 2 * D_MODEL
RMS_EPS = 1e-6
NEG_INF = -1e30

kernel_name = 'hybrid_conv_swa_sink_convffn_adaln_step'


def _rmsnorm(x, g):
    x32 = x.astype(jnp.float32)
    y = x32 * lax.rsqrt(jnp.mean(x32 * x32, axis=-1, keepdims=True) + RMS_EPS)
    return (y * g.astype(jnp.float32)).astype(x.dtype)


def _causal_conv(u, buf, w):
    t = u.shape[1]
    full = jnp.concatenate([buf.astype(u.dtype), u], axis=1)
    y = full[:, 0:t] * w[0]
    for tap in range(1, CONV_WIDTH):
        y = y + full[:, tap:tap + t] * w[tap]
    return y, full[:, t:]


def _alibi_slopes():
    h = jnp.arange(1, N_HEADS + 1, dtype=jnp.float32)
    return jnp.exp2(-8.0 * h / N_HEADS).reshape(N_KV_HEADS, GROUP)


def _sink_attend(q, k, v, dist, valid, sinks):
    s = jnp.einsum('...qhgd,...khd->...hgqk', q, k).astype(jnp.float32) * (HEAD_DIM ** -0.5)
    s = s - _alibi_slopes()[:, :, None, None] * dist[..., None, None, :, :]
    s = jnp.where(valid[..., None, None, :, :], s, NEG_INF)
    sink = sinks.astype(jnp.float32)[:, :, None, None]
    m = jnp.maximum(jnp.max(s, axis=-1, keepdims=True), sink)
    p = jnp.exp(s - m)
    p = p / (jnp.sum(p, axis=-1, keepdims=True) + jnp.exp(sink - m))
    return jnp.einsum('...hgqk,...khd->...qhgd', p.astype(v.dtype), v)


def _attend_prompt(q, k, v, sinks):
    b, s_len = q.shape[0], q.shape[1]
    nb = s_len // BLOCK
    qb = q.reshape(b, nb, BLOCK, N_KV_HEADS, GROUP, HEAD_DIM)
    kb = k.reshape(b, nb, BLOCK, N_KV_HEADS, HEAD_DIM)
    vb = v.reshape(b, nb, BLOCK, N_KV_HEADS, HEAD_DIM)

    def with_prev(t):
        prev = jnp.concatenate([jnp.zeros_like(t[:, :1]), t[:, :-1]], axis=1)
        return jnp.concatenate([prev, t], axis=2)

    kk, vv = with_prev(kb), with_prev(vb)
    i = jnp.arange(BLOCK)[:, None]
    j = jnp.arange(2 * BLOCK)[None, :]
    dist = BLOCK + i - j
    kpos = jnp.arange(nb)[:, None, None] * BLOCK - BLOCK + j[None]
    valid = (dist >= 0) & (dist <= WINDOW) & (kpos >= 0)
    o = _sink_attend(qb, kk, vv, dist.astype(jnp.float32), valid, sinks)
    return o.reshape(b, s_len, Q_WIDTH), k[:, -WINDOW:], v[:, -WINDOW:]


def _make_attend_sample(k_buf, v_buf):
    def attend(q, k, v, sinks):
        b, t = q.shape[0], q.shape[1]
        lb = k_buf.shape[1]
        kk = jnp.concatenate([k_buf.astype(k.dtype), k], axis=1)
        vv = jnp.concatenate([v_buf.astype(v.dtype), v], axis=1)
        i = jnp.arange(t)[:, None]
        j = jnp.arange(lb + t)[None, :]
        dist = lb + i - j
        valid = (dist >= 0) & (dist <= WINDOW)
        o = _sink_attend(q, kk, vv, dist.astype(jnp.float32), valid, sinks)
        return o.reshape(b, t, Q_WIDTH), kk[:, t:], vv[:, t:]
    return attend


def _layer(x, c, conv_buf, ffn_buf, attend, norm1_g, norm2_g, w_ada, b_ada, w_in, conv_w,
           w_conv_out, attn_sinks, w_attn_out, w_mix_out, w_up, ffn_conv_w, w_down):
    b, t = x.shape[0], x.shape[1]
    mod = jax.nn.silu(c) @ w_ada + b_ada
    sh1, sc1, g1, sh2, sc2, g2 = [m[:, None, :] for m in jnp.split(mod, 6, axis=-1)]

    h = _rmsnorm(x, norm1_g) * (1.0 + sc1) + sh1
    proj = h @ w_in
    cuts = np.cumsum([D_CONV, D_CONV, D_CONV, Q_WIDTH, KV_WIDTH, KV_WIDTH, D_MODEL]).tolist()
    b_gate, c_gate, xa, q, k, v, ga, gb = jnp.split(proj, cuts, axis=-1)
    u = c_gate * xa
    uc, new_conv = _causal_conv(u, conv_buf, conv_w)
    ya = (b_gate * uc) @ w_conv_out
    q = q.reshape(b, t, N_KV_HEADS, GROUP, HEAD_DIM)
    k = k.reshape(b, t, N_KV_HEADS, HEAD_DIM)
    v = v.reshape(b, t, N_KV_HEADS, HEAD_DIM)
    o, k_state, v_state = attend(q, k, v, attn_sinks.reshape(N_KV_HEADS, GROUP))
    yb = o @ w_attn_out
    mixed = jax.nn.sigmoid(ga) * ya + jax.nn.sigmoid(gb) * yb
    x = x + g1 * (mixed @ w_mix_out)

    h2 = _rmsnorm(x, norm2_g) * (1.0 + sc2) + sh2
    a, val = jnp.split(h2 @ w_up, 2, axis=-1)
    ac, new_ffn = _causal_conv(a, ffn_buf, ffn_conv_w)
    x = x + g2 * ((jax.nn.gelu(ac, approximate=False) * val) @ w_down)
    return x, new_conv, k_state, v_state, new_ffn


def setup_inputs(seed: int = 0) -> dict:
    key = jax.random.key(seed)
    ks = jax.random.split(key, 24)
    f32 = jnp.float32

    def nrm(k, shape, scale=1.0):
        return jax.random.normal(k, shape, f32) * scale

    buf_len = min(WINDOW, PAST_LEN)
    return {
        'x_prompt': nrm(ks[0], (BATCH, SEQ, D_MODEL)),
        'x_sample': nrm(ks[1], (DEC_BATCH, DEC_SEQ, D_MODEL)),
        'c_prompt': nrm(ks[2], (BATCH, D_MODEL)),
        'c_sample': nrm(ks[3], (DEC_BATCH, D_MODEL)),
        'state_conv': nrm(ks[4], (DEPTH, DEC_BATCH, CONV_WIDTH - 1, D_CONV)),
        'cache_k_win': nrm(ks[5], (DEPTH, DEC_BATCH, buf_len, N_KV_HEADS, HEAD_DIM)),
        'cache_v_win': nrm(ks[6], (DEPTH, DEC_BATCH, buf_len, N_KV_HEADS, HEAD_DIM)),
        'state_ffn_conv': nrm(ks[7], (DEPTH, DEC_BATCH, CONV_WIDTH - 1, D_FF)),
        'norm1_g': 1.0 + nrm(ks[8], (DEPTH, D_MODEL), 0.01),
        'norm2_g': 1.0 + nrm(ks[9], (DEPTH, D_MODEL), 0.01),
        'w_ada': nrm(ks[10], (DEPTH, D_MODEL, 6 * D_MODEL), 0.3 * D_MODEL ** -0.5),
        'b_ada': nrm(ks[11], (DEPTH, 6 * D_MODEL), 0.01),
        'w_in': nrm(ks[12], (DEPTH, D_MODEL, IN_WIDTH), D_MODEL ** -0.5),
        'conv_w': nrm(ks[13], (DEPTH, CONV_WIDTH, D_CONV), CONV_WIDTH ** -0.5),
        'w_conv_out': nrm(ks[14], (DEPTH, D_CONV, D_MODEL), D_CONV ** -0.5),
        'attn_sinks': nrm(ks[15], (DEPTH, N_HEADS)),
        'w_attn_out': nrm(ks[16], (DEPTH, Q_WIDTH, D_MODEL), Q_WIDTH ** -0.5),
        'w_mix_out': nrm(ks[17], (DEPTH, D_MODEL, D_MODEL), D_MODEL ** -0.5),
        'w_up': nrm(ks[18], (DEPTH, D_MODEL, 2 * D_FF), D_MODEL ** -0.5),
        'ffn_conv_w': nrm(ks[19], (DEPTH, CONV_WIDTH, D_FF), CONV_WIDTH ** -0.5),
        'w_down': nrm(ks[20], (DEPTH, D_FF, D_MODEL), D_FF ** -0.5),
        'final_g': 1.0 + nrm(ks[21], (D_MODEL,), 0.01),
    }


def reference(x_prompt, x_sample, c_prompt, c_sample, state_conv, cache_k_win, cache_v_win,
              state_ffn_conv, norm1_g, norm2_g, w_ada, b_ada, w_in, conv_w, w_conv_out,
              attn_sinks, w_attn_out, w_mix_out, w_up, ffn_conv_w, w_down, final_g):
    xp, xs = x_prompt, x_sample
    conv_p, k_p, v_p, ffn_p = [], [], [], []
    conv_s, k_s, v_s, ffn_s = [], [], [], []
    for layer in range(DEPTH):
        params = (norm1_g[layer], norm2_g[layer], w_ada[layer], b_ada[layer], w_in[layer],
                  conv_w[layer], w_conv_out[layer], attn_sinks[layer], w_attn_out[layer],
                  w_mix_out[layer], w_up[layer], ffn_conv_w[layer], w_down[layer])
        zero_conv = jnp.zeros((xp.shape[0], CONV_WIDTH - 1, D_CONV), xp.dtype)
        zero_ffn = jnp.zeros((xp.shape[0], CONV_WIDTH - 1, D_FF), xp.dtype)
        xp, cp_, kp_, vp_, fp_ = _layer(xp, c_prompt, zero_conv, zero_ffn, _attend_prompt, *params)
        attend_s = _make_attend_sample(cache_k_win[layer], cache_v_win[layer])
        xs, cs_, ks_, vs_, fs_ = _layer(xs, c_sample, state_conv[layer], state_ffn_conv[layer],
                                        attend_s, *params)
        conv_p.append(cp_); k_p.append(kp_); v_p.append(vp_); ffn_p.append(fp_)
        conv_s.append(cs_); k_s.append(ks_); v_s.append(vs_); ffn_s.append(fs_)
    y_prompt = _rmsnorm(xp, final_g)
    y_sample = _rmsnorm(xs, final_g)
    return (y_prompt, y_sample,
            jnp.stack(conv_p, axis=0), jnp.stack(k_p, axis=0), jnp.stack(v_p, axis=0),
            jnp.stack(ffn_p, axis=0),
            jnp.stack(conv_s, axis=0), jnp.stack(k_s, axis=0), jnp.stack(v_s, axis=0),
            jnp.stack(ffn_s, axis=0))
```

```python
import numpy as np
from contextlib import ExitStack
import concourse.bass as bass
import concourse.mybir as mybir
from concourse.bass_utils import run_bass_kernel_spmd

F32 = mybir.dt.float32
BF16 = mybir.dt.bfloat16
AF = mybir.ActivationFunctionType
ALU = mybir.AluOpType

NCORES = 8
D = 1024
DFF = 2816
NJF = 22
TP = 2048
TS = 128
TT_ALL = TP + TS
CHUNKS = [
    dict(base=0, ntok=768, groups=[(0, 512, "p"), (512, 256, "p")]),
    dict(base=768, ntok=768, groups=[(0, 512, "p"), (512, 256, "p")]),
    dict(base=1536, ntok=640, groups=[(0, 512, "p"), (512, 128, "s")]),
]
TC = 768
RING_ELEMS = 4096
NRING = 3


class Op:
    __slots__ = ("eng", "fn", "deps", "dma", "signal", "count", "idx")


class Prog:
    ENG = ("pe", "act", "dve", "pool", "sp")

    def __init__(self, nc):
        self.nc = nc
        self.q = {e: [] for e in self.ENG}
        self.lastw = {}
        self.readers = {}
        self.dma_groups = {}
        self.disabled = False

    def op(self, eng, fn, r=(), w=(), dma=None):
        if self.disabled:
            return None
        o = Op()
        o.eng, o.fn, o.dma, o.signal, o.count = eng, fn, dma, False, 0
        deps = set()
        for k in r:
            x = self.lastw.get(k)
            if x is not None:
                deps.add(x)
            if k.startswith("ps"):
                for x in self.readers.get(k, ()):
                    if x.eng != eng:
                        deps.add(x)
        for k in w:
            x = self.lastw.get(k)
            if x is not None:
                deps.add(x)
            for x in self.readers.get(k, ()):
                deps.add(x)
        rset = set(r)
        keep = []
        latest = {}
        for x in deps:
            if x.dma is not None:
                keep.append(x)
                continue
            if x.eng == eng and eng == "pe":
                continue
            y = latest.get(x.eng)
            if y is None or x.idx > y.idx:
                latest[x.eng] = x
        keep.extend(latest.values())
        for x in keep:
            x.signal = True
        o.deps = keep
        o.idx = len(self.q[eng])
        if dma is not None:
            g = self.dma_groups.setdefault(dma, [])
            g.append(o)
            o.count = 16 * len(g)
        wset = set(w)
        for k in wset:
            self.lastw[k] = o
            self.readers[k] = []
        for k in rset:
            if k not in wset:
                self.readers.setdefault(k, []).append(o)
        self.q[eng].append(o)
        return o

    def emit(self, stack):
        nc = self.nc
        sems = {e: stack.enter_context(nc.semaphore("s_" + e)) for e in self.ENG}
        gsem = {g: stack.enter_context(nc.semaphore("d_" + g)) for g in self.dma_groups}
        for e in self.ENG:
            c = 0
            for o in self.q[e]:
                if o.dma is None and o.signal:
                    c += 1
                    o.count = c
        block = stack.enter_context(nc.Block())
        handles = {"pe": block.tensor, "act": block.scalar, "dve": block.vector,
                   "pool": block.gpsimd, "sp": block.sync}
        for e in self.ENG:
            ops = self.q[e]

            def body(eng, ops=ops, e=e):
                seen = {}
                for o in ops:
                    need = {}
                    for x in o.deps:
                        s = gsem[x.dma] if x.dma is not None else sems[x.eng]
                        key = id(s)
                        if x.count > seen.get(key, 0):
                            if key not in need or need[key][1] < x.count:
                                need[key] = (s, x.count)
                    for key, (s, cnt) in need.items():
                        eng.wait_ge(s, cnt)
                        seen[key] = cnt
                    ins = o.fn(eng)
                    if o.dma is not None:
                        ins.then_inc(gsem[o.dma], 16)
                    elif o.signal:
                        ins.then_inc(sems[e], 1)
                if e == "sp":
                    for g, lst in self.dma_groups.items():
                        eng.wait_ge(gsem[g], 16 * len(lst))
            handles[e](body)


def build_nc():
    nc = bass.Bass("TRN2", target_bir_lowering=False)

    def din(name, shape):
        return nc.dram_tensor(name, list(shape), F32, kind="ExternalInput").ap()

    def dout(name, shape):
        return nc.dram_tensor(name, list(shape), F32, kind="ExternalOutput").ap()

    xin = din("xin", [TT_ALL, D])
    cvec = din("cvec", [32, D])
    st_conv = din("st_conv", [32, D])
    st_ffn = din("st_ffn", [32, DFF])
    ck = din("ck", [16, 128, 256])
    cv = din("cv", [16, 128, 256])
    ckT_d = din("ckT", [128, 8, 1024])
    rowsA = din("rowsA", [128, 128])
    rowsB = din("rowsB", [128, 128])
    sinkT_d = din("sinkT", [128, 8])
    ident_d = din("ident", [128, 128])
    Dp_d = din("Dp", [128, 4096])
    Dn_d = din("Dn", [128, 2048])
    Dc_d = din("Dc", [128, 128])
    wada_t = din("wada_t", [12, 128, 4096])
    wq_t = din("wq_t", [2, 128, 4096])
    wk_t = din("wk_t", [128, 4096])
    wv_t = din("wv_t", [128, 2048])
    wcv_t = din("wcv_t", [8, 128, 3072])
    wmg_t = din("wmg_t", [8, 128, 4096])
    wmix_t = din("wmix_t", [2, 128, 4096])
    wup_t = din("wup_t", [NJF, 128, 2048])
    wdn_t = din("wdn_t", [8, 128, 2816])

    y_d = dout("y", [TT_ALL, D])
    o_conv_p = dout("o_conv_p", [2, D])
    o_k_p = dout("o_k_p", [128, 256])
    o_v_p = dout("o_v_p", [128, 256])
    o_ffn_p = dout("o_ffn_p", [2, DFF])
    o_conv_s = dout("o_conv_s", [32, D])
    o_k_s = dout("o_k_s", [16, 128, 256])
    o_v_s = dout("o_v_s", [16, 128, 256])
    o_ffn_s = dout("o_ffn_s", [32, DFF])

    with ExitStack() as st:
        def sb(name, shape, dt):
            return st.enter_context(nc.sbuf_tensor(name, list(shape), dt))

        xT = sb("xT", [128, 8, TC], F32)
        hT = sb("hT", [128, 8, TC], BF16)
        AR = sb("arena", [128, 24, TC], BF16)
        kT = sb("kT", [128, 4, 128 + TC], BF16)
        vsb = sb("vsb", [128, 7, 256], BF16)
        ring = [sb("ring%d" % i, [128, RING_ELEMS], BF16) for i in range(NRING)]
        Dp = sb("Dp_sb", [128, 4096], BF16)
        Dn = sb("Dn_sb", [128, 2048], BF16)
        Dc = sb("Dc_sb", [128, 128], BF16)
        modT = sb("modT", [128, 48, 32], F32)
        xs = [sb("xs%d" % i, [128, 1024], F32) for i in range(2)]
        xl = [sb("xl%d" % i, [128, 1024], F32) for i in range(2)]
        pT = [sb("pT%d" % i, [128, 1024], BF16) for i in range(2)]
        C_sb = [sb("C_sb%d" % i, [128, 512], F32) for i in range(2)]
        B_sb = [sb("B_sb%d" % i, [128, 512], BF16) for i in range(2)]
        uext = [sb("uext%d" % i, [128, 2 + TC], BF16) for i in range(3)]
        ues = sb("ues", [128, 8, 16, 10], BF16)
        aes = sb("aes", [128, NJF, 16, 10], BF16)
        uhalo = sb("uhalo", [128, 8, 2], BF16)
        ahalo = sb("ahalo", [128, NJF, 2], BF16)
        sga = [sb("sga%d" % i, [128, 512], BF16) for i in range(2)]
        sgb = [sb("sgb%d" % i, [128, 512], BF16) for i in range(2)]
        t1s = [sb("t1_%d" % i, [128, 512], BF16) for i in range(2)]
        t2s = [sb("t2_%d" % i, [128, 512], BF16) for i in range(2)]
        ge = [sb("ge%d" % i, [128, 512], BF16) for i in range(2)]
        rs = sb("rs", [128, 512], F32)
        rstd = sb("rstd", [128, 512], F32)
        tmp = [sb("tmp%d" % i, [128, 512], F32) for i in range(2)]
        rden = sb("rden", [128, 1024], F32)
        xn = rden.bitcast(BF16)[:, 0:1024]
        stok = sb("stok", [128, 32], F32)
        dgc = [sb("dgc%d" % i, [128, 3, 128], BF16) for i in range(2)]
        cst = sb("cst", [128, 8, 34], F32)
        fst = sb("fst", [128, NJF, 34], F32)
        kstage = [sb("kstage%d" % i, [128, 256], F32) for i in range(2)]
        vstage = [sb("vstage%d" % i, [128, 256], F32) for i in range(2)]
        kcd = sb("kcd", [128, 2, 4, 128], BF16)
        kcT = sb("kcT", [128, 2, 4, 128], BF16)
        ident_f = sb("ident_f", [128, 128], F32)
        ident_b = sb("ident_b", [128, 128], BF16)
        ones_b = sb("ones_b", [128, 128], BF16)
        vecA = sb("vecA", [128, 128], F32)
        vecB = sb("vecB", [128, 128], F32)
        esink = sb("esink", [128, 8], F32)
        scT = sb("scT", [128, 8, 32], BF16)
        xT_flat = xT[:, :, :].rearrange("p a t -> p (a t)")
        hT_flat = hT[:, :, :].rearrange("p a t -> p (a t)")
        stf_sb = xT_flat[0:32, 0:2816]
        stc_sb = xT_flat[0:32, 2816:3840]
        cs_f = xT_flat[0:32, 3840:4864]
        rowsA_sb = xT_flat[:, 4864:4992]
        rowsB_sb = xT_flat[:, 4992:5120]
        cs_b = hT_flat[0:32, 0:1024]
        XT_ALL = ["xT:%d:%d" % (kc, g) for kc in range(8) for g in range(2)]
        HT_ALL = ["hT:%d:%d" % (kc, g) for kc in range(8) for g in range(2)]
        ps = st.enter_context(nc.psum_tensor("ps", [128, 8, 512], F32))
        ps_bf = ps.bitcast(BF16)

        ar_flat = AR[:, 16:24, :].rearrange("p a t -> p (a t)")
        vc = ar_flat[:, 0:4096].rearrange("p (s d) -> p s d", d=256)
        pTc = ar_flat[:, 4096:6144].rearrange("p (s c) -> p s c", c=128)
        pTn_t = sb("pTn", [128, 4, 512], BF16)

        P = Prog(nc)
        state = dict(bank=0, pair=0, ring=0, xs=0)
        import os
        KSTOP = os.environ.get("KSTOP", "")

        def ckpt(name):
            if KSTOP == name:
                P.disabled = True

        def alloc():
            b = state["bank"]
            state["bank"] = (b + 1) % 8
            return b

        def alloc_pair(lo=0, n=4):
            p = state["pair"] % n
            state["pair"] = (p + 1) % n
            return lo + p

        def PS(b):
            return "ps%d" % b

        def ring_load(src_ap, nel):
            r_ = state["ring"]
            state["ring"] = (r_ + 1) % NRING
            P.op("pool", lambda e, r_=r_: e.dma_start(out=ring[r_][:, 0:nel], in_=src_ap),
                 w=["ring%d" % r_], dma="ring%d" % r_)
            return r_

        def ARK(idx, gi):
            return "AR:%d:%d" % (idx, gi)

        GT_ALL = [ARK(i, g) for i in range(16, 24) for g in range(2)]

        P.op("sp", lambda e: e.dma_start(out=ident_f[:], in_=ident_d), w=["ident_f"], dma="c0")
        P.op("sp", lambda e: e.dma_start(out=rowsA_sb, in_=rowsA), w=["rowsA"], dma="c1")
        P.op("sp", lambda e: e.dma_start(out=rowsB_sb, in_=rowsB), w=["rowsB"], dma="c2")
        P.op("sp", lambda e: e.dma_start(out=esink[:], in_=sinkT_d), w=["esink"], dma="c3")
        P.op("sp", lambda e: e.dma_start(out=cs_f, in_=cvec), w=["cs_f"], dma="c4")
        P.op("sp", lambda e: e.dma_start(out=stc_sb, in_=st_conv), w=["stc"], dma="c5")
        P.op("sp", lambda e: e.dma_start(out=stf_sb, in_=st_ffn), w=["stf"], dma="c6")
        P.op("dve", lambda e: e.tensor_copy(out=ident_b[:], in_=ident_f[:]), r=["ident_f"], w=["ident_b"])
        P.op("dve", lambda e: e.memset(ones_b[:], 1.0), w=["ones_b"])
        P.op("dve", lambda e: e.memset(uext[0][:, 0:2], 0.0), w=["uh0"])
        P.op("dve", lambda e: e.memset(uext[1][:, 0:2], 0.0), w=["uh1"])
        P.op("act", lambda e: e.activation(out=esink[:], in_=esink[:], func=AF.Exp), r=["esink"], w=["esink"])
        b0 = alloc()
        P.op("pe", lambda e: e.transpose(ps[:, b0, 0:128], rowsA_sb, ident_f[:]), r=["rowsA", "ident_f"] + XT_ALL, w=[PS(b0)])
        P.op("dve", lambda e: e.tensor_copy(out=vecA[:], in_=ps[:, b0, 0:128]), r=[PS(b0)], w=["vecA"])
        b1 = alloc()
        P.op("pe", lambda e: e.transpose(ps[:, b1, 0:128], rowsB_sb, ident_f[:]), r=["rowsB", "ident_f"] + XT_ALL, w=[PS(b1)])
        P.op("dve", lambda e: e.tensor_copy(out=vecB[:], in_=ps[:, b1, 0:128]), r=[PS(b1)], w=["vecB"])
        bp = alloc_pair()
        for j in range(8):
            P.op("pe", lambda e, j=j, bp=bp: e.transpose(ps[:, 2 * bp + j // 4, (j % 4) * 128:(j % 4) * 128 + 32],
                                                  stc_sb[:, j * 128:(j + 1) * 128], ident_f[0:32, 0:32]),
                 r=["stc", "ident_f"] + XT_ALL, w=[PS(2 * bp), PS(2 * bp + 1)])
        P.op("dve", lambda e, bp=bp: e.tensor_copy(
            out=ues[:, :, :, 0:2],
            in_=ps[:, 2 * bp:2 * bp + 2, :].rearrange("p a (j c) -> p (a j) c", c=128)[:, :, 0:32].rearrange("p j (s r) -> p j s r", r=2)),
            r=[PS(2 * bp), PS(2 * bp + 1)], w=["ues_h"])
        for rnd in range(3):
            bp = alloc_pair()
            j0 = rnd * 8
            nj = min(8, NJF - j0)
            for jj in range(nj):
                P.op("pe", lambda e, jj=jj, j0=j0, bp=bp: e.transpose(
                    ps[:, 2 * bp + jj // 4, (jj % 4) * 128:(jj % 4) * 128 + 32],
                    stf_sb[:, (j0 + jj) * 128:(j0 + jj + 1) * 128], ident_f[0:32, 0:32]),
                    r=["stf", "ident_f"] + XT_ALL, w=[PS(2 * bp), PS(2 * bp + 1)])
            P.op("dve", lambda e, j0=j0, nj=nj, bp=bp: e.tensor_copy(
                out=aes[:, j0:j0 + nj, :, 0:2],
                in_=ps[:, 2 * bp:2 * bp + 2, :].rearrange("p a (j c) -> p (a j) c", c=128)[:, 0:nj, 0:32].rearrange("p j (s r) -> p j s r", r=2)),
                r=[PS(2 * bp), PS(2 * bp + 1)], w=["aes_h"])
        P.op("act", lambda e: e.activation(out=cs_b, in_=cs_f, func=AF.Silu), r=["cs_f"] + XT_ALL, w=["cs_b"] + HT_ALL)
        b2 = alloc()
        for kc in range(8):
            P.op("pe", lambda e, kc=kc: e.transpose(ps_bf[:, b2, kc * 32:(kc + 1) * 32], cs_b[:, kc * 128:(kc + 1) * 128],
                                                    ident_b[0:32, 0:32]),
                 r=["cs_b", "ident_b"] + HT_ALL, w=[PS(b2)])
        P.op("dve", lambda e: e.tensor_copy(out=scT[:].rearrange("p k c -> p (k c)"), in_=ps_bf[:, b2, 0:256]), r=[PS(b2)], w=["scT"])
        ckpt("setup")

        def sview(ap2d):
            return ap2d.rearrange("p (s t) -> p s t", t=8)

        def mod_b(idx):
            return modT[:, idx, 1:17].unsqueeze(2).to_broadcast([128, 16, 8])

        def norm_A(ch, gi):
            g0, n, kind = ch["groups"][gi]
            hk_ = ["hT:%d:%d" % (kc, gi) for kc in range(8)]
            xk_ = ["xT:%d:%d" % (kc, gi) for kc in range(8)]
            P.op("act", lambda e, g0=g0, n=n: e.activation(out=hT[:, :, g0:g0 + n], in_=xT[:, :, g0:g0 + n], func=AF.Square),
                 r=xk_, w=hk_)

        def norm_stats(ch, gi):
            g0, n, kind = ch["groups"][gi]
            hk_ = ["hT:%d:%d" % (kc, gi) for kc in range(8)]
            b = alloc()
            for kc in range(8):
                P.op("pe", lambda e, kc=kc, g0=g0, n=n, b=b: e.matmul(ps[:, b, 0:n], lhsT=ones_b[:], rhs=hT[:, kc, g0:g0 + n],
                                                                     start=(kc == 0), stop=(kc == 7)),
                     r=["ones_b", hk_[kc]], w=[PS(b)])
            P.op("act", lambda e, n=n, b=b: e.activation(out=rs[:, 0:n], in_=ps[:, b, 0:n], func=AF.Ln, scale=1.0 / D, bias=1e-6),
                 r=[PS(b)], w=["rs"])
            P.op("act", lambda e, n=n: e.activation(out=rstd[:, 0:n], in_=rs[:, 0:n], func=AF.Exp, scale=-0.5), r=["rs"], w=["rstd"])

        def norm_B(ch, gi, s_idx, sh_idx):
            g0, n, kind = ch["groups"][gi]
            hk_ = ["hT:%d:%d" % (kc, gi) for kc in range(8)]
            xk_ = ["xT:%d:%d" % (kc, gi) for kc in range(8)]
            norm_stats(ch, gi)
            for kc in range(8):
                tb = kc % 2
                P.op("dve", lambda e, kc=kc, g0=g0, n=n, tb=tb: e.tensor_tensor(
                    out=tmp[tb][:, 0:n], in0=xT[:, kc, g0:g0 + n], in1=rstd[:, 0:n], op=ALU.mult),
                    r=[xk_[kc], "rstd"], w=["tmp%d" % tb])
                if kind == "p":
                    P.op("act", lambda e, kc=kc, g0=g0, n=n, tb=tb: e.activation(
                        out=hT[:, kc, g0:g0 + n], in_=tmp[tb][:, 0:n], func=AF.Identity,
                        scale=modT[:, s_idx + kc, 0:1], bias=modT[:, sh_idx + kc, 0:1]),
                        r=["tmp%d" % tb, MK(s_idx), MK(sh_idx)], w=[hk_[kc]])
                else:
                    P.op("dve", lambda e, kc=kc, n=n, tb=tb: e.tensor_tensor(
                        out=sview(tmp[tb][:, 0:n]), in0=sview(tmp[tb][:, 0:n]), in1=mod_b(s_idx + kc), op=ALU.mult),
                        r=["tmp%d" % tb, MK(s_idx)], w=["tmp%d" % tb])
                    P.op("dve", lambda e, kc=kc, g0=g0, n=n, tb=tb: e.tensor_tensor(
                        out=sview(hT[:, kc, g0:g0 + n]), in0=sview(tmp[tb][:, 0:n]), in1=mod_b(sh_idx + kc), op=ALU.add),
                        r=["tmp%d" % tb, MK(sh_idx)], w=[hk_[kc]])

        def norm_mod(ci, ch, s_idx, sh_idx):
            for gi in range(len(ch["groups"])):
                norm_A(ch, gi)
                norm_B(ch, gi, s_idx, sh_idx)

        def resid_add(b, n, j, g0, gi, kind, g_idx):
            xk = "xT:%d:%d" % (j, gi)
            if kind == "p":
                P.op("dve", lambda e: e.scalar_tensor_tensor(
                    out=xT[:, j, g0:g0 + n], in0=ps[:, b, 0:n], scalar=modT[:, g_idx + j, 0:1], in1=xT[:, j, g0:g0 + n],
                    op0=ALU.mult, op1=ALU.add), r=[PS(b), MK(g_idx), xk], w=[xk])
            else:
                P.op("dve", lambda e: e.tensor_tensor(out=sview(tmp[0][:, 0:n]), in0=sview(ps[:, b, 0:n]), in1=mod_b(g_idx + j),
                                                      op=ALU.mult), r=[PS(b), MK(g_idx)], w=["tmp0"])
                P.op("dve", lambda e: e.tensor_tensor(out=xT[:, j, g0:g0 + n], in0=tmp[0][:, 0:n], in1=xT[:, j, g0:g0 + n],
                                                      op=ALU.add), r=["tmp0", xk], w=[xk])

        def proj8(b, n, wfn, rfn, wkeys, rkeys):
            for kc in range(8):
                l_, r2_ = wfn(kc), rfn(kc)
                P.op("pe", lambda e, kc=kc, l_=l_, r2_=r2_, b=b, n=n: e.matmul(ps[:, b, 0:n], lhsT=l_, rhs=r2_, start=(kc == 0), stop=(kc == 7)),
                     r=[wkeys, rkeys[kc]], w=[PS(b)])

        evac_tog = [0]

        def evac_copy(out_ap, in_ap, r, w):
            evac_tog[0] ^= 1
            if evac_tog[0]:
                P.op("act", lambda e: e.activation(out=out_ap, in_=in_ap, func=AF.Copy), r=r, w=w)
            else:
                P.op("dve", lambda e: e.tensor_copy(out=out_ap, in_=in_ap), r=r, w=w)

        def MK(idx):
            return "modT%d" % (idx // 16)

        def ada_unit(quarter, uu, bank=None):
            bq = alloc() if bank is None else bank
            u = quarter * 4 + uu
            r_ = ring_load(wada_t[u], 4096)
            wv_ = ring[r_][:, :].rearrange("p (c k n) -> p c k n", c=4, k=8)
            for cc in range(4):
                for kc in range(8):
                    P.op("pe", lambda e, wv_=wv_, cc=cc, kc=kc, bq=bq: e.matmul(
                        ps[:, bq, cc * 32:(cc + 1) * 32], lhsT=wv_[:, cc, kc, :], rhs=scT[:, kc, :],
                        start=(kc == 0), stop=(kc == 7)),
                        r=["ring%d" % r_, "scT"], w=[PS(bq)])
            c0 = quarter * 16 + uu * 4
            P.op("dve", lambda e, c0=c0, bq=bq: e.tensor_tensor(
                out=modT[:, c0:c0 + 4, :],
                in0=ps[:, bq, 0:128].rearrange("p (c s) -> p c s", s=32),
                in1=vecA[:, c0:c0 + 4].unsqueeze(2).to_broadcast([128, 4, 32]),
                op=ALU.add), r=[PS(bq), "vecA"], w=["modT%d" % quarter])

        def ada_quarter(quarter):
            for uu in range(4):
                ada_unit(quarter, uu)

        def mod_scale(c0, gcol):
            P.op("dve", lambda e, c0=c0: e.tensor_scalar(out=modT[:, c0:c0 + 8, :], in0=modT[:, c0:c0 + 8, :], scalar1=1.0,
                                                         scalar2=None, op0=ALU.add), r=[MK(c0)], w=[MK(c0)])
            P.op("dve", lambda e, c0=c0, gcol=gcol: e.tensor_tensor(
                out=modT[:, c0:c0 + 8, :], in0=modT[:, c0:c0 + 8, :],
                in1=vecA[:, gcol:gcol + 8].unsqueeze(2).to_broadcast([128, 8, 32]), op=ALU.mult),
                r=[MK(c0), "vecA"], w=[MK(c0)])

        def tile_row0(ci, tt):
            ch = CHUNKS[ci]
            ntiles = ch["ntok"] // 128
            lastc = (ci == len(CHUNKS) - 1)
            return TP if (lastc and tt == ntiles - 1) else ch["base"] + tt * 128

        def xl_load(ci, tt, s_):
            row0 = tile_row0(ci, tt)
            P.op("sp", lambda e, s_=s_, row0=row0: e.dma_start(out=xl[s_][:], in_=xin[row0:row0 + 128, :]),
                 w=["xl%d" % s_], dma="xl%d" % s_)

        def phase_B(ci, ch, tiles=None):
            groups = ch["groups"]
            ntiles = ch["ntok"] // 128
            tiles = list(range(ntiles)) if tiles is None else list(tiles)
            for i in range(min(2, len(tiles))):
                xl_load(ci, tiles[i], i % 2)
            for i, tt in enumerate(tiles):
                s_ = i % 2
                bp = alloc_pair()
                for kc in range(8):
                    P.op("pe", lambda e, kc=kc, bp=bp, s_=s_: e.transpose(
                        ps[:, 2 * bp + kc // 4, (kc % 4) * 128:(kc % 4 + 1) * 128], xl[s_][:, kc * 128:(kc + 1) * 128], ident_f[:]),
                        r=["xl%d" % s_, "ident_f"], w=[PS(2 * bp), PS(2 * bp + 1)])
                if i + 2 < len(tiles):
                    xl_load(ci, tiles[i + 2], s_)
                gi = [k_ for k_, (g0, n, k) in enumerate(groups) if g0 <= tt * 128 < g0 + n][0]
                evac_copy(xT[:, :, tt * 128:(tt + 1) * 128],
                          ps[:, 2 * bp:2 * bp + 2, :].rearrange("p a (k c) -> p (a k) c", c=128),
                          [PS(2 * bp), PS(2 * bp + 1)], ["xT:%d:%d" % (kc, gi) for kc in range(8)])

        def pre_norm(ci, tt, s_, defer_evac=False):
            groups = CHUNKS[ci]["groups"]
            gi = [k_ for k_, (g0, n, k) in enumerate(groups) if g0 <= tt * 128 < g0 + n][0]
            c0 = 3 * (tt % 8)
            xl_load(ci, tt, s_)
            P.op("act", lambda e, s_=s_, c0=c0: e.activation(out=xn, in_=xl[s_][:], func=AF.Square, accum_out=stok[:, c0:c0 + 1]),
                 r=["xl%d" % s_], w=["rden", "stok"])
            P.op("act", lambda e, c0=c0: e.activation(out=stok[:, c0 + 1:c0 + 2], in_=stok[:, c0:c0 + 1], func=AF.Ln, scale=1.0 / D, bias=1e-6),
                 r=["stok"], w=["stok"])
            P.op("act", lambda e, c0=c0: e.activation(out=stok[:, c0 + 2:c0 + 3], in_=stok[:, c0 + 1:c0 + 2], func=AF.Exp, scale=-0.5),
                 r=["stok"], w=["stok"])
            P.op("act", lambda e, s_=s_, c0=c0: e.activation(out=xn, in_=xl[s_][:], func=AF.Identity, scale=stok[:, c0 + 2:c0 + 3]),
                 r=["xl%d" % s_, "stok"], w=["rden"])
            b = alloc()
            for kc in range(8):
                P.op("pe", lambda e, kc=kc, b=b: e.transpose(ps_bf[:, b, kc * 128:(kc + 1) * 128], xn[:, kc * 128:(kc + 1) * 128], ident_b[:]),
                     r=["rden", "ident_b"], w=[PS(b)])
            if defer_evac:
                return lambda: pre_norm_evac(ci, tt, gi, b)
            pre_norm_evac(ci, tt, gi, b)

        def pre_norm_evac(ci, tt, gi, b):
            for kc in range(8):
                outv = hT[:, kc, tt * 128:(tt + 1) * 128]
                inv = ps_bf[:, b, kc * 128:(kc + 1) * 128]
                if kc % 2 == 0:
                    P.op("act", lambda e, kc=kc, outv=outv, inv=inv: e.activation(
                        out=outv, in_=inv, func=AF.Identity, scale=modT[:, 8 + kc, 0:1], bias=modT[:, kc, 0:1]),
                        r=[PS(b), MK(8), MK(0)], w=["hT:%d:%d" % (kc, gi)])
                else:
                    P.op("dve", lambda e, kc=kc, outv=outv, inv=inv: e.tensor_scalar(
                        out=outv, in0=inv, scalar1=modT[:, 8 + kc, 0:1], scalar2=modT[:, kc, 0:1], op0=ALU.mult, op1=ALU.add),
                        r=[PS(b), MK(8), MK(0)], w=["hT:%d:%d" % (kc, gi)])

        fillers = []

        def fill(k):
            for _ in range(k):
                if fillers:
                    fillers.pop(0)()

        def make_B_steps(ci, tiles):
            ch = CHUNKS[ci]
            groups = ch["groups"]
            st_ = dict(nl=0)

            def issue_load():
                i = st_["nl"]
                if i < len(tiles):
                    xl_load(ci, tiles[i], i % 2)
                    st_["nl"] = i + 1

            def step(i):
                tt = tiles[i]
                s_ = i % 2
                while st_["nl"] <= i:
                    issue_load()
                bp = alloc_pair()
                for kc in range(8):
                    P.op("pe", lambda e, kc=kc, bp=bp, s_=s_: e.transpose(
                        ps[:, 2 * bp + kc // 4, (kc % 4) * 128:(kc % 4 + 1) * 128], xl[s_][:, kc * 128:(kc + 1) * 128], ident_f[:]),
                        r=["xl%d" % s_, "ident_f"], w=[PS(2 * bp), PS(2 * bp + 1)])
                if st_["nl"] <= i + 2:
                    issue_load()
                gi = [k_ for k_, (g0, n, k) in enumerate(groups) if g0 <= tt * 128 < g0 + n][0]
                evac_copy(xT[:, :, tt * 128:(tt + 1) * 128],
                          ps[:, 2 * bp:2 * bp + 2, :].rearrange("p a (k c) -> p (a k) c", c=128),
                          [PS(2 * bp), PS(2 * bp + 1)], ["xT:%d:%d" % (kc, gi) for kc in range(8)])

            def prefetch():
                issue_load()
                issue_load()

            return prefetch, [(lambda i=i: step(i)) for i in range(len(tiles))]

        ada_unit(0, 0)
        phase_B(0, CHUNKS[0])
        for uu in range(1, 4):
            ada_unit(0, uu)
        mod_scale(8, 48)
        norm_mod(0, CHUNKS[0], 8, 0)
        P.op("pool", lambda e: e.dma_start(out=Dp[:], in_=Dp_d), w=["Dp"], dma="m0")
        P.op("pool", lambda e: e.dma_start(out=Dn[:], in_=Dn_d), w=["Dn"], dma="m1")
        P.op("pool", lambda e: e.dma_start(out=Dc[:], in_=Dc_d), w=["Dc"], dma="m2")
        P.op("sp", lambda e: e.dma_start(out=o_k_s[:, 0:120, :], in_=ck[:, 8:128, :]), dma="okc")
        P.op("sp", lambda e: e.dma_start(out=o_v_s[:, 0:120, :], in_=cv[:, 8:128, :]), dma="ovc")

        for ci, ch in enumerate(CHUNKS):
            groups = ch["groups"]
            ntiles = ch["ntok"] // 128
            last = (ci == len(CHUNKS) - 1)

            ckpt("c%dB" % ci)

            HK = lambda kc, gi: "hT:%d:%d" % (kc, gi)

            ckpt("c%dC" % ci)
            wq_v = []
            for u in range(2):
                r_ = ring_load(wq_t[u], 4096)
                wq_v.append((r_, ring[r_][:, :].rearrange("p (c k n) -> p c k n", c=4, k=8)))
            r_k = ring_load(wk_t, 4096)
            wkv = ring[r_k][:, :].rearrange("p (c k n) -> p c k n", c=4, k=8)
            r_ = r_k

            def q_unit(j, gi):
                g0, n, kind = groups[gi]
                rq_, wv_ = wq_v[j // 4]
                jj = j % 4
                b = alloc()
                proj8(b, n, lambda kc, wv_=wv_, jj=jj: wv_[:, jj, kc, :], lambda kc, g0=g0, n=n: hT[:, kc, g0:g0 + n],
                      "ring%d" % rq_, [HK(kc, gi) for kc in range(8)])
                evac_copy(AR[:, j, g0:g0 + n], ps[:, b, 0:n], [PS(b)], [ARK(j, gi)])

            def k_unit(hk, gi):
                g0, n, kind = groups[gi]
                b = alloc()
                proj8(b, n, lambda kc, hk=hk: wkv[:, hk, kc, :], lambda kc, g0=g0, n=n: hT[:, kc, g0:g0 + n],
                      "ring%d" % r_k, [HK(kc, gi) for kc in range(8)])
                evac_copy(kT[:, hk, 128 + g0:128 + g0 + n], ps[:, b, 0:n], [PS(b)], ["kT:%d:%d" % (hk, gi)])

            if last:
                nf = max(1, (len(fillers) + 11) // 12)
                for j in range(8):
                    q_unit(j, 0)
                    fill(nf)
                for hk in range(4):
                    k_unit(hk, 0)
                    fill(nf)
                nrest = ntiles - 1
                fill(max(0, len(fillers) - nrest))
                for j in range(8):
                    q_unit(j, 1)
                    fill(1)
                for hk in range(4):
                    k_unit(hk, 1)
                fill(len(fillers))
            else:
                nf = max(1, (len(fillers) + 23) // 24)
                for j in range(8):
                    for gi in range(len(groups)):
                        q_unit(j, gi)
                        fill(nf)
                for hk in range(4):
                    for gi in range(len(groups)):
                        k_unit(hk, gi)
                        fill(nf)
                fill(len(fillers))
            if last and "nokwin" not in os.environ.get("KDBG", ""):
                for wi, tt in enumerate((ntiles - 2, ntiles - 1)):
                    gi = len(groups) - 2 + wi
                    b = alloc()
                    for hk in range(4):
                        for kc in range(8):
                            rk_ = wkv[:, hk, kc, 0:64]
                            P.op("pe", lambda e, hk=hk, kc=kc, tt=tt, b=b, rk_=rk_: e.matmul(
                                ps[:, b, hk * 64:(hk + 1) * 64], lhsT=hT[:, kc, tt * 128:(tt + 1) * 128], rhs=rk_,
                                start=(kc == 0), stop=(kc == 7)), r=["ring%d" % r_, HK(kc, gi)], w=[PS(b)])
                    P.op("act", lambda e, wi=wi, b=b: e.activation(out=kstage[wi][:], in_=ps[:, b, 0:256], func=AF.Copy),
                         r=[PS(b)], w=["kstage%d" % wi])
                P.op("sp", lambda e: e.dma_start(out=o_k_p, in_=kstage[0][:]), r=["kstage0"], dma="ok")
                for s in range(16 if "nosmall" not in os.environ.get("KDBG", "") else 0):
                    P.op("sp", lambda e, s=s: e.dma_start(out=o_k_s[s, 120:128, :], in_=kstage[1][s * 8:(s + 1) * 8, :]),
                         r=["kstage1"], dma="ok")
            r_ = ring_load(wv_t, 2048)
            wvv = ring[r_][:, 0:2048].rearrange("p (k n) -> p k n", k=8)
            for tt in range(ntiles):
                gi = [i for i, (g0, n, k) in enumerate(groups) if g0 <= tt * 128 < g0 + n][0]
                b = alloc()
                for kc in range(8):
                    rv_ = wvv[:, kc, :]
                    P.op("pe", lambda e, kc=kc, tt=tt, b=b, rv_=rv_: e.matmul(ps[:, b, 0:256], lhsT=hT[:, kc, tt * 128:(tt + 1) * 128],
                                                                    rhs=rv_, start=(kc == 0), stop=(kc == 7)),
                         r=["ring%d" % r_, HK(kc, gi)], w=[PS(b)])
                P.op("dve", lambda e, tt=tt, b=b: e.tensor_copy(out=vsb[:, 1 + tt, :], in_=ps[:, b, 0:256]), r=[PS(b)], w=["v:%d" % (1 + tt)])
                if last and tt >= ntiles - 2 and "novwin" not in os.environ.get("KDBG", ""):
                    wi = tt - (ntiles - 2)
                    P.op("act", lambda e, wi=wi, b=b: e.activation(out=vstage[wi][:], in_=ps[:, b, 0:256], func=AF.Copy),
                         r=[PS(b)], w=["vstage%d" % wi])
                    if wi == 0:
                        P.op("sp", lambda e: e.dma_start(out=o_v_p, in_=vstage[0][:]), r=["vstage0"], dma="ov")
                    else:
                        for s in range(16 if "nosmall" not in os.environ.get("KDBG", "") else 0):
                            P.op("sp", lambda e, s=s: e.dma_start(out=o_v_s[s, 120:128, :], in_=vstage[1][s * 8:(s + 1) * 8, :]),
                                 r=["vstage1"], dma="ov")

            ckpt("c%dD" % ci)
            def gi_of(tok):
                return [i for i, (g0, n, k) in enumerate(groups) if g0 <= tok < g0 + n][0]

            nqb = ntiles - 1 if last else ntiles
            NUMP, DENP = 2, 3
            numv = ps[:, 4:6, :].rearrange("p a (c q) -> p (a c) q", q=128)
            denv = ps[:, 6:8, :].rearrange("p a (c q) -> p (a c) q", q=128)

            def normalize(qs, gi):
                P.op("dve", lambda e: e.tensor_tensor(out=rden[:].rearrange("p (c q) -> p c q", q=128), in0=denv,
                                                      in1=esink[:, :].unsqueeze(2).to_broadcast([128, 8, 128]), op=ALU.add),
                     r=[PS(6), PS(7), "esink"], w=["rden"])
                P.op("act", lambda e: e.activation(out=rden[:], in_=rden[:], func=AF.Ln), r=["rden"], w=["rden"])
                P.op("act", lambda e: e.activation(out=rden[:], in_=rden[:], func=AF.Exp, scale=-1.0), r=["rden"], w=["rden"])
                P.op("dve", lambda e: e.tensor_tensor(out=AR[:, 8:16, qs:qs + 128], in0=numv,
                                                      in1=rden[:].rearrange("p (c q) -> p c q", q=128), op=ALU.mult),
                     r=[PS(4), PS(5), "rden"], w=[ARK(8 + c, gi) for c in range(8)])

            num4 = ps[:, 4, :].rearrange("p (c q) -> p c q", q=128)
            den4 = ps[:, 5, :].rearrange("p (c q) -> p c q", q=128)

            def att_S(qb, hk):
                qs = qb * 128
                gi = gi_of(qs)
                has_prev = not (ci == 0 and qb == 0)
                kbs = ([0] if has_prev else []) + [1]
                kgi_prev = gi_of(qs - 128) if qb > 0 else None
                sp_ = 0
                pslot = (qb * 4 + hk) % 2
                for kb in kbs:
                    kcol = qs + kb * 128
                    if kb == 1:
                        kkey = "kT:%d:%d" % (hk, gi)
                    elif qb > 0:
                        kkey = "kT:%d:%d" % (hk, kgi_prev)
                    else:
                        kkey = "kTprev"
                    for hh in range(4):
                        h = 4 * hk + hh
                        half, chunk = h % 2, h // 2
                        c0_ = kb * 256 + (hh // 2) * 128
                        P.op("pe", lambda e, sp_=sp_, c0_=c0_, half=half, chunk=chunk, hk=hk, kcol=kcol, qs=qs: e.matmul(
                            ps[:, 2 * sp_ + half, c0_:c0_ + 128],
                            lhsT=kT[half * 64:(half + 1) * 64, hk, kcol:kcol + 128],
                            rhs=AR[half * 64:(half + 1) * 64, chunk, qs:qs + 128], start=True, stop=True),
                            r=[kkey, ARK(chunk, gi)], w=[PS(2 * sp_ + half)])
                regions = [(0, 1024)] if has_prev else [(256, 512), (768, 1024)]
                for (lo, hi) in regions:
                    P.op("act", lambda e, sp_=sp_, lo=lo, hi=hi, pslot=pslot: e.activation(
                        out=pT[pslot][:, lo:hi], in_=ps[:, 2 * sp_:2 * sp_ + 2, :].rearrange("p a n -> p (a n)")[:, lo:hi],
                        func=AF.Exp, scale=0.125), r=[PS(2 * sp_), PS(2 * sp_ + 1)], w=["pT%d" % pslot])
                    P.op("dve", lambda e, lo=lo, hi=hi, pslot=pslot, hk=hk: e.tensor_tensor(
                        out=pT[pslot][:, lo:hi], in0=pT[pslot][:, lo:hi], in1=Dp[:, hk * 1024 + lo:hk * 1024 + hi], op=ALU.mult),
                        r=["pT%d" % pslot, "Dp"], w=["pT%d" % pslot])

            def att_PV(qb, hk):
                qs = qb * 128
                gi = gi_of(qs)
                has_prev = not (ci == 0 and qb == 0)
                kbs = ([0] if has_prev else []) + [1]
                pslot = (qb * 4 + hk) % 2
                hp = hk // 2
                lc0 = 2 * hk - 4 * hp
                for half in range(2):
                    for (dst, isnum) in ((num4, True), (den4, False)):
                        for ki, kb in enumerate(kbs):
                            vt = qb + kb
                            if isnum:
                                lhs = vsb[:, vt, hk * 64:(hk + 1) * 64]
                                rk = ["v:%d" % vt, "pT%d" % pslot]
                            else:
                                lhs = ones_b[:, 0:64]
                                rk = ["ones_b", "pT%d" % pslot]
                            wk_ = [PS(4)] if isnum else [PS(5)]
                            lastk = (ki == len(kbs) - 1)
                            c0_ = half * 512 + kb * 256
                            P.op("pe", lambda e, dst=dst, half=half, lc0=lc0, lhs=lhs, pslot=pslot, c0_=c0_, ki=ki, lastk=lastk: e.matmul(
                                dst[half * 64:(half + 1) * 64, lc0:lc0 + 2, :], lhsT=lhs, rhs=pT[pslot][:, c0_:c0_ + 256],
                                start=(ki == 0), stop=lastk), r=rk, w=wk_)
                if hk % 2 == 1:
                    P.op("dve", lambda e, hp=hp: e.tensor_tensor(
                        out=rden[:, 0:512].rearrange("p (c q) -> p c q", q=128), in0=den4,
                        in1=esink[:, 4 * hp:4 * hp + 4].unsqueeze(2).to_broadcast([128, 4, 128]), op=ALU.add),
                        r=[PS(5), "esink"], w=["rden"])

                    def norm_fn(hp=hp, qs=qs, gi=gi):
                        P.op("act", lambda e: e.activation(out=rden[:, 0:512], in_=rden[:, 0:512], func=AF.Ln), r=["rden"], w=["rden"])
                        P.op("act", lambda e: e.activation(out=rden[:, 0:512], in_=rden[:, 0:512], func=AF.Exp, scale=-1.0), r=["rden"], w=["rden"])
                        P.op("dve", lambda e: e.tensor_tensor(
                            out=AR[:, 8 + 4 * hp:12 + 4 * hp, qs:qs + 128], in0=num4,
                            in1=rden[:, 0:512].rearrange("p (c q) -> p c q", q=128), op=ALU.mult),
                            r=[PS(4), "rden"], w=[ARK(8 + 4 * hp + c, gi) for c in range(4)])
                    return norm_fn
                return None

            if last:
                qs = (ntiles - 1) * 128
                gi = len(groups) - 1
                P.op("pool", lambda e: e.dma_start(out=vc, in_=cv.rearrange("s k d -> k s d")), w=GT_ALL, dma="vc")
                for hk in range(4):
                    sp_ = alloc_pair(0, 2)
                    for hh in range(4):
                        h = 4 * hk + hh
                        half, chunk = h % 2, h // 2
                        cc_ = hh // 2
                        P.op("pe", lambda e, sp_=sp_, cc_=cc_, half=half, chunk=chunk, hk=hk, qs=qs: e.matmul(
                            ps[:, 2 * sp_ + half, cc_ * 128:(cc_ + 1) * 128], lhsT=kT[half * 64:(half + 1) * 64, hk, 128 + qs:128 + qs + 128],
                            rhs=AR[half * 64:(half + 1) * 64, chunk, qs:qs + 128], start=True, stop=True),
                            r=["kT:%d:%d" % (hk, gi), ARK(chunk, gi)], w=[PS(2 * sp_ + half)])
                    P.op("act", lambda e, sp_=sp_, hk=hk: e.activation(out=pTn_t[:, hk, :].rearrange("p (a n) -> p a n", a=2),
                                                                       in_=ps[:, 2 * sp_:2 * sp_ + 2, 0:256], func=AF.Exp, scale=0.125),
                         r=[PS(2 * sp_), PS(2 * sp_ + 1)], w=["pTn%d" % hk])
                    P.op("dve", lambda e, hk=hk: e.tensor_tensor(out=pTn_t[:, hk, :], in0=pTn_t[:, hk, :], in1=Dn[:, hk * 512:(hk + 1) * 512],
                                                                 op=ALU.mult), r=["pTn%d" % hk, "Dn"], w=["pTn%d" % hk])
                kslots = [kcd[:], kcT[:], pT[0][:, :].rearrange("p (s h k) -> p s h k", s=2, h=4),
                          pT[1][:, :].rearrange("p (s h k) -> p s h k", s=2, h=4)]
                kkeys = ["kcb0", "kcb1", "pT0", "pT1"]
                for bt in range(8):
                    s0 = bt * 2
                    ksl = bt % 4
                    kcb = kslots[ksl]
                    kkey = kkeys[ksl]
                    P.op("pool", lambda e, bt=bt, kcb=kcb: e.dma_start(out=kcb.rearrange("p s h k -> p (s h k)"), in_=ckT_d[:, bt, :]),
                         w=[kkey], dma="kcs%d" % ksl)
                    sbp = alloc_pair(0, 2)
                    for sl in range(2):
                        s = s0 + sl
                        for hk in range(4):
                            for half in range(2):
                                c0 = sl * 64 + hk * 16
                                P.op("pe", lambda e, sl=sl, hk=hk, half=half, c0=c0, s=s, sbp=sbp, qs=qs, kcb=kcb: e.matmul(
                                    ps[:, 2 * sbp + half, c0:c0 + 16].rearrange("p (c q) -> p c q", q=8),
                                    lhsT=kcb[half * 64:(half + 1) * 64, sl, hk, :],
                                    rhs=AR[half * 64:(half + 1) * 64, 2 * hk:2 * hk + 2, qs + s * 8:qs + s * 8 + 8],
                                    start=True, stop=True), r=[kkey, ARK(2 * hk, gi), ARK(2 * hk + 1, gi)], w=[PS(2 * sbp + half)])
                    P.op("act", lambda e, s0=s0, sbp=sbp: e.activation(
                        out=pTc[:, s0:s0 + 2, :].rearrange("p s (h c) -> p s h c", h=2),
                        in_=ps[:, 2 * sbp:2 * sbp + 2, 0:128].rearrange("p h (s c) -> p s h c", s=2),
                        func=AF.Exp, scale=0.125), r=[PS(2 * sbp), PS(2 * sbp + 1)] + GT_ALL, w=GT_ALL)
                    P.op("dve", lambda e, s0=s0: e.tensor_tensor(out=pTc[:, s0:s0 + 2, :], in0=pTc[:, s0:s0 + 2, :],
                                                                 in1=Dc[:, :].unsqueeze(1).to_broadcast([128, 2, 128]), op=ALU.mult),
                         r=GT_ALL + ["Dc"], w=GT_ALL)
                vts = ntiles
                for hk in range(4):
                    for hh in range(4):
                        h = 4 * hk + hh
                        half, cc = hh % 2, hh // 2
                        chunk = 2 * hk + cc
                        idx = half * 8 + hk * 2 + cc
                        for (dst, isnum) in ((numv, True), (denv, False)):
                            wk_ = [PS(4 + chunk // 4)] if isnum else [PS(6 + chunk // 4)]
                            lhs = vsb[:, vts, hk * 64:(hk + 1) * 64] if isnum else ones_b[:, 0:64]
                            P.op("pe", lambda e, dst=dst, half=half, chunk=chunk, lhs=lhs, hk=hk, hh=hh: e.matmul(
                                dst[half * 64:(half + 1) * 64, chunk, :], lhsT=lhs, rhs=pTn_t[:, hk, (half * 2 + hh // 2) * 128:(half * 2 + hh // 2 + 1) * 128],
                                start=True, stop=False), r=["v:%d" % vts, "pTn%d" % hk, "ones_b"], w=wk_)
                            for s in range(16):
                                lhs2 = vc[:, s, hk * 64:(hk + 1) * 64] if isnum else ones_b[:, 0:64]
                                P.op("pe", lambda e, dst=dst, half=half, chunk=chunk, lhs2=lhs2, s=s, idx=idx: e.matmul(
                                    dst[half * 64:(half + 1) * 64, chunk, s * 8:(s + 1) * 8], lhsT=lhs2,
                                    rhs=pTc[:, s, idx * 8:(idx + 1) * 8], start=False, stop=(s == 15)),
                                    r=GT_ALL + ["ones_b"], w=wk_)
                normalize(qs, gi)

            ckpt("c%dE" % ci)
            conv_state = {}

            def conv_prologue(j):
                r_ = ring_load(wcv_t[j], 3072)
                wv_ = ring[r_][:, 0:3072].rearrange("p (k t n) -> p k t n", k=8, t=3)
                ds_ = j % 2
                for tap in range(3):
                    P.op("dve", lambda e, ds_=ds_, tap=tap, j=j: e.tensor_scalar(
                        out=dgc[ds_][:, tap, :], in0=ident_b[:], scalar1=vecA[:, 72 + tap * 8 + j:73 + tap * 8 + j], scalar2=None,
                        op0=ALU.mult), r=["ident_b", "vecA"], w=["dgc%d" % ds_])
                us = j % 2
                if ci > 0:
                    P.op("dve", lambda e, us=us, j=j: e.tensor_copy(out=uext[us][:, 0:2], in_=uhalo[:, j, :]), r=["uhalo:%d" % j], w=["uh%d" % us])
                conv_state[j] = (r_, wv_, ds_, us)

            def conv_seg1(j, gi):
                r_, wv_, ds_, us = conv_state[j]
                g0, n, kind = groups[gi]
                bC, bX = 2, 3
                for t_, b in ((1, bC), (2, bX)):
                    proj8(b, n, lambda kc, t_=t_: wv_[:, kc, t_, :], lambda kc, g0=g0, n=n: hT[:, kc, g0:g0 + n],
                          "ring%d" % r_, [HK(kc, gi) for kc in range(8)])
                cs_ = (j * 2 + gi) % 2
                P.op("act", lambda e, cs_=cs_, n=n, bC=bC: e.activation(out=C_sb[cs_][:, 0:n], in_=ps[:, bC, 0:n], func=AF.Copy),
                     r=[PS(bC)], w=["C_sb%d" % cs_])
                ukey = "u:%d:%d" % (us, gi)
                if kind == "p":
                    P.op("dve", lambda e, us=us, g0=g0, n=n, bX=bX, cs_=cs_: e.tensor_tensor(
                        out=uext[us][:, 2 + g0:2 + g0 + n], in0=ps[:, bX, 0:n], in1=C_sb[cs_][:, 0:n], op=ALU.mult),
                        r=[PS(bX), "C_sb%d" % cs_], w=[ukey])
                    if last and gi == len(groups) - 2:
                        P.op("dve", lambda e, n=n, bX=bX, cs_=cs_, j=j: e.tensor_tensor(
                            out=cst[:, j, 32:34], in0=ps[:, bX, n - 2:n], in1=C_sb[cs_][:, n - 2:n], op=ALU.mult),
                            r=[PS(bX), "C_sb%d" % cs_], w=["cst:%d" % j])
                else:
                    P.op("dve", lambda e, n=n, bX=bX, cs_=cs_, j=j: e.tensor_tensor(
                        out=ues[:, j, :, 2:10], in0=sview(ps[:, bX, 0:n]), in1=sview(C_sb[cs_][:, 0:n]), op=ALU.mult),
                        r=[PS(bX), "C_sb%d" % cs_, "ues_h"], w=[ukey])
                    P.op("dve", lambda e, n=n, bX=bX, cs_=cs_, j=j: e.tensor_tensor(
                        out=cst[:, j, 0:32].rearrange("p (s r) -> p s r", r=2), in0=sview(ps[:, bX, 0:n])[:, :, 6:8],
                        in1=sview(C_sb[cs_][:, 0:n])[:, :, 6:8], op=ALU.mult),
                        r=[PS(bX), "C_sb%d" % cs_], w=["cst:%d" % j])

            def conv_seg2(j, gi):
                r_, wv_, ds_, us = conv_state[j]
                g0, n, kind = groups[gi]
                bB, bU = 6, 7
                cs_ = (j * 2 + gi) % 2
                ukey = "u:%d:%d" % (us, gi)
                proj8(bB, n, lambda kc: wv_[:, kc, 0, :], lambda kc, g0=g0, n=n: hT[:, kc, g0:g0 + n],
                      "ring%d" % r_, [HK(kc, gi) for kc in range(8)])
                P.op("act", lambda e, cs_=cs_, n=n, bB=bB: e.activation(out=B_sb[cs_][:, 0:n], in_=ps[:, bB, 0:n], func=AF.Copy),
                     r=[PS(bB)], w=["B_sb%d" % cs_])
                for tap in range(3):
                    if kind == "p":
                        rhs = uext[us][:, g0 + tap:g0 + tap + n]
                        rk = [ukey, "uh%d" % us] + (["u:%d:%d" % (us, gi - 1)] if gi > 0 else [])
                    else:
                        rhs = ues[:, j, :, tap:tap + 8]
                        rk = [ukey, "ues_h"]
                    P.op("pe", lambda e, ds_=ds_, tap=tap, rhs=rhs, bU=bU, n=n: e.matmul(
                        ps[:, bU, 0:n], lhsT=dgc[ds_][:, tap, :], rhs=rhs, start=(tap == 0), stop=(tap == 2)),
                        r=["dgc%d" % ds_] + rk, w=[PS(bU)])
                P.op("dve", lambda e, j=j, g0=g0, n=n, bU=bU, cs_=cs_: e.tensor_tensor(
                    out=AR[:, 16 + j, g0:g0 + n], in0=ps[:, bU, 0:n], in1=B_sb[cs_][:, 0:n], op=ALU.mult),
                    r=[PS(bU), "B_sb%d" % cs_], w=[ARK(16 + j, gi)])

            def conv_epilogue(j):
                r_, wv_, ds_, us = conv_state[j]
                if not last:
                    lt = ch["ntok"]
                    P.op("dve", lambda e, us=us, j=j, lt=lt: e.tensor_copy(out=uhalo[:, j, :], in_=uext[us][:, lt:lt + 2]),
                         r=["u:%d:%d" % (us, len(groups) - 1)], w=["uhalo:%d" % j])

            att_units = [(qb, hk) for qb in range(nqb) for hk in range(4)]
            segs = []
            for j in range(8):
                for gi in range(len(groups)):
                    segs.append((1, j, gi))
                    segs.append((2, j, gi))
            na, nsg = len(att_units), len(segs)
            ai = 0
            pend = [None]

            def flush_norm():
                if pend[0] is not None:
                    pend[0]()
                    pend[0] = None

            for si, (kind_, j, gi) in enumerate(segs):
                want = ((si + 1) * na + nsg - 1) // nsg
                a = None
                if ai < min(want, na):
                    a = att_units[ai]
                    ai += 1
                    att_S(*a)
                    flush_norm()
                if kind_ == 1:
                    if gi == 0:
                        conv_prologue(j)
                    conv_seg1(j, gi)
                else:
                    conv_seg2(j, gi)
                    if gi == len(groups) - 1:
                        conv_epilogue(j)
                if a is not None:
                    flush_norm()
                    pend[0] = att_PV(*a)
            while ai < na:
                a = att_units[ai]
                ai += 1
                att_S(*a)
                flush_norm()
                pend[0] = att_PV(*a)
            flush_norm()
            if not last:
                lt = ch["ntok"] - 128
                lgi = gi_of(lt)
                P.op("dve", lambda e, lt=lt: e.tensor_copy(out=kT[:, :, 0:128], in_=kT[:, :, 128 + lt:128 + lt + 128]),
                     r=["kT:%d:%d" % (hk, lgi) for hk in range(4)] + ["kT:%d:%d" % (hk, 0) for hk in range(4)], w=["kTprev"])
                P.op("dve", lambda e, nt=ntiles: e.tensor_copy(out=vsb[:, 0, :], in_=vsb[:, nt, :]),
                     r=["v:%d" % ntiles, "v:1"], w=["v:0"])
            if last:
                bp = alloc_pair(0, 2)
                for j in range(8):
                    P.op("pe", lambda e, j=j, bp=bp: e.transpose(ps[0:34, 2 * bp + j // 4, (j % 4) * 128:(j % 4 + 1) * 128], cst[:, j, :], ident_f[:]),
                         r=["cst:%d" % j, "ident_f"], w=[PS(2 * bp), PS(2 * bp + 1)])
                s_ = state["xs"]
                state["xs"] ^= 1
                P.op("act", lambda e, bp=bp, s_=s_: e.activation(out=xs[s_][0:34, :], in_=ps[0:34, 2 * bp:2 * bp + 2, :].rearrange("p a n -> p (a n)"),
                                                                 func=AF.Copy), r=[PS(2 * bp), PS(2 * bp + 1)], w=["xs%d" % s_])
                P.op("sp", lambda e, s_=s_: e.dma_start(out=o_conv_s, in_=xs[s_][0:32, :]), r=["xs%d" % s_], dma="xs%d" % s_)
                P.op("sp", lambda e, s_=s_: e.dma_start(out=o_conv_p, in_=xs[s_][32:34, :]), r=["xs%d" % s_], dma="xs%d" % s_)

            ckpt("c%dF" % ci)
            for j in range(8):
                if ci == 0:
                    ada_unit(1 + j // 4, j % 4)
                    if j == 7:
                        mod_scale(32, 56)
                r_ = ring_load(wmg_t[j], 4096)
                wv_ = ring[r_][:, :].rearrange("p (k t n) -> p k t n", k=8, t=4)
                for gi, (g0, n, kind) in enumerate(groups):
                    bA, bBb, bYa, bYb = alloc(), alloc(), alloc(), alloc()
                    hk_ = [HK(kc, gi) for kc in range(8)]
                    proj8(bA, n, lambda kc: wv_[:, kc, 0, :], lambda kc, g0=g0, n=n: hT[:, kc, g0:g0 + n], "ring%d" % r_, hk_)
                    proj8(bBb, n, lambda kc: wv_[:, kc, 1, :], lambda kc, g0=g0, n=n: hT[:, kc, g0:g0 + n], "ring%d" % r_, hk_)
                    proj8(bYa, n, lambda kc: wv_[:, kc, 2, :], lambda kc, g0=g0, n=n: AR[:, 16 + kc, g0:g0 + n], "ring%d" % r_,
                          [ARK(16 + kc, gi) for kc in range(8)])
                    proj8(bYb, n, lambda kc: wv_[:, kc, 3, :], lambda kc, g0=g0, n=n: AR[:, 8 + kc, g0:g0 + n], "ring%d" % r_,
                          [ARK(8 + kc, gi) for kc in range(8)])
                    ss_ = (j * 2 + gi) % 2
                    P.op("act", lambda e, ss_=ss_, n=n, bA=bA: e.activation(out=sga[ss_][:, 0:n], in_=ps[:, bA, 0:n], func=AF.Sigmoid),
                         r=[PS(bA)], w=["sga%d" % ss_])
                    P.op("act", lambda e, ss_=ss_, n=n, bBb=bBb: e.activation(out=sgb[ss_][:, 0:n], in_=ps[:, bBb, 0:n], func=AF.Sigmoid),
                         r=[PS(bBb)], w=["sgb%d" % ss_])
                    P.op("dve", lambda e, ss_=ss_, n=n, bYa=bYa: e.tensor_tensor(out=t1s[ss_][:, 0:n], in0=ps[:, bYa, 0:n], in1=sga[ss_][:, 0:n], op=ALU.mult),
                         r=[PS(bYa), "sga%d" % ss_], w=["t1_%d" % ss_])
                    P.op("dve", lambda e, ss_=ss_, n=n, bYb=bYb: e.tensor_tensor(out=t2s[ss_][:, 0:n], in0=ps[:, bYb, 0:n], in1=sgb[ss_][:, 0:n], op=ALU.mult),
                         r=[PS(bYb), "sgb%d" % ss_], w=["t2_%d" % ss_])
                    P.op("dve", lambda e, ss_=ss_, j=j, g0=g0, n=n: e.tensor_tensor(out=AR[:, j, g0:g0 + n], in0=t1s[ss_][:, 0:n], in1=t2s[ss_][:, 0:n], op=ALU.add),
                         r=["t1_%d" % ss_, "t2_%d" % ss_], w=[ARK(j, gi)])

            ckpt("c%dG" % ci)
            wmix_v = []
            for u in range(2):
                r_ = ring_load(wmix_t[u], 4096)
                wmix_v.append((r_, ring[r_][:, :].rearrange("p (c k n) -> p c k n", c=4, k=8)))
            for gi, (g0, n, kind) in enumerate(groups):
                for j in range(8):
                    r_, wv_ = wmix_v[j // 4]
                    jj = j % 4
                    b = alloc()
                    proj8(b, n, lambda kc, jj=jj, wv_=wv_: wv_[:, jj, kc, :], lambda kc, g0=g0, n=n: AR[:, kc, g0:g0 + n],
                          "ring%d" % r_, [ARK(kc, gi) for kc in range(8)])
                    resid_add(b, n, j, g0, gi, kind, 16)
                    if gi > 0 and j == 3:
                        norm_B(ch, gi - 1, 32, 24)
                norm_A(ch, gi)
            ckpt("c%dH" % ci)
            norm_B(ch, len(groups) - 1, 32, 24)

            ckpt("c%dI" % ci)
            ffn_state = {}

            def ffn_s1(jf, gi, gs_):
                g0, n, kind = groups[gi]
                us = jf % 3
                if gi == 0:
                    r_ = ring_load(wup_t[jf], 2048)
                    wv_ = ring[r_][:, 0:2048].rearrange("p (k t n) -> p k t n", k=8, t=2)
                    ffn_state[jf] = (r_, wv_)
                    if ci > 0:
                        P.op("dve", lambda e, us=us, jf=jf: e.tensor_copy(out=uext[us][:, 0:2], in_=ahalo[:, jf, :]), r=["ahalo:%d" % jf], w=["uh%d" % us])
                    else:
                        P.op("dve", lambda e, us=us: e.memset(uext[us][:, 0:2], 0.0), w=["uh%d" % us])
                r_, wv_ = ffn_state[jf]
                bA, bV = alloc(), alloc()
                hk_ = [HK(kc, gi) for kc in range(8)]
                proj8(bA, n, lambda kc: wv_[:, kc, 0, :], lambda kc, g0=g0, n=n: hT[:, kc, g0:g0 + n], "ring%d" % r_, hk_)
                proj8(bV, n, lambda kc: wv_[:, kc, 1, :], lambda kc, g0=g0, n=n: hT[:, kc, g0:g0 + n], "ring%d" % r_, hk_)
                ukey = "u:%d:%d" % (us, gi)
                P.op("act", lambda e, gs_=gs_, n=n, bA=bA, jf=jf: e.activation(
                    out=C_sb[gs_][:, 0:n], in_=ps[:, bA, 0:n], func=AF.Identity, scale=vecB[:, 2 * NJF + jf:2 * NJF + jf + 1]),
                    r=[PS(bA), "vecB"], w=["C_sb%d" % gs_])
                if kind == "p":
                    P.op("act", lambda e, us=us, g0=g0, n=n, bA=bA: e.activation(out=uext[us][:, 2 + g0:2 + g0 + n], in_=ps[:, bA, 0:n], func=AF.Copy),
                         r=[PS(bA)], w=[ukey])
                    if last and gi == len(groups) - 2:
                        P.op("act", lambda e, n=n, bA=bA, jf=jf: e.activation(out=fst[:, jf, 32:34], in_=ps[:, bA, n - 2:n], func=AF.Copy),
                             r=[PS(bA)], w=["fst:%d" % jf])
                else:
                    P.op("act", lambda e, n=n, bA=bA, jf=jf: e.activation(out=aes[:, jf, :, 2:10], in_=sview(ps[:, bA, 0:n]), func=AF.Copy),
                         r=[PS(bA), "aes_h"], w=[ukey])
                    P.op("act", lambda e, n=n, bA=bA, jf=jf: e.activation(
                        out=fst[:, jf, 0:32].rearrange("p (s r) -> p s r", r=2), in_=sview(ps[:, bA, 0:n])[:, :, 6:8], func=AF.Copy),
                        r=[PS(bA)], w=["fst:%d" % jf])
                if gi == len(groups) - 1 and not last:
                    lt = ch["ntok"]
                    P.op("dve", lambda e, us=us, jf=jf, lt=lt: e.tensor_copy(out=ahalo[:, jf, :], in_=uext[us][:, lt:lt + 2]),
                         r=[ukey], w=["ahalo:%d" % jf])
                return bV

            def ffn_s2(jf, gi, gs_):
                g0, n, kind = groups[gi]
                us = jf % 3
                ukey = "u:%d:%d" % (us, gi)
                if kind == "p":
                    taps = [uext[us][:, g0 + tap:g0 + tap + n] for tap in range(3)]
                    accv, outv = C_sb[gs_][:, 0:n], B_sb[gs_][:, 0:n]
                    rk = [ukey, "uh%d" % us] + (["u:%d:%d" % (us, gi - 1)] if gi > 0 else [])
                else:
                    taps = [aes[:, jf, :, tap:tap + 8] for tap in range(3)]
                    accv, outv = sview(C_sb[gs_][:, 0:n]), sview(B_sb[gs_][:, 0:n])
                    rk = [ukey, "aes_h"]
                wcol = [vecB[:, tap * NJF + jf:tap * NJF + jf + 1] for tap in range(3)]
                P.op("dve", lambda e, accv=accv, taps=taps, wcol=wcol: e.scalar_tensor_tensor(
                    out=accv, in0=taps[1], scalar=wcol[1], in1=accv, op0=ALU.mult, op1=ALU.add),
                    r=rk + ["vecB", "C_sb%d" % gs_], w=["C_sb%d" % gs_])
                P.op("dve", lambda e, accv=accv, outv=outv, taps=taps, wcol=wcol: e.scalar_tensor_tensor(
                    out=outv, in0=taps[0], scalar=wcol[0], in1=accv, op0=ALU.mult, op1=ALU.add),
                    r=rk + ["vecB", "C_sb%d" % gs_], w=["B_sb%d" % gs_])
                P.op("act", lambda e, gs_=gs_, n=n: e.activation(out=ge[gs_][:, 0:n], in_=B_sb[gs_][:, 0:n], func=AF.Gelu),
                     r=["B_sb%d" % gs_], w=["ge%d" % gs_])

            def ffn_s3(jf, gi, gs_, bV):
                g0, n, kind = groups[gi]
                P.op("dve", lambda e, gs_=gs_, jf=jf, g0=g0, n=n, bV=bV: e.tensor_tensor(
                    out=AR[:, jf, g0:g0 + n], in0=ps[:, bV, 0:n], in1=ge[gs_][:, 0:n], op=ALU.mult),
                    r=[PS(bV), "ge%d" % gs_], w=[ARK(jf, gi)])

            ng = len(groups)
            units = [(jf, 0) for jf in range(3)] + [(jf, gi) for jf in range(3) for gi in range(1, ng)] + \
                    [(jf, gi) for jf in range(3, NJF) for gi in range(ng)]
            bvs = {}
            for t in range(len(units) + 2):
                if t < len(units):
                    bvs[t] = ffn_s1(*units[t], t % 2)
                if 0 <= t - 1 < len(units):
                    ffn_s2(*units[t - 1], (t - 1) % 2)
                if 0 <= t - 2 < len(units):
                    ffn_s3(*units[t - 2], (t - 2) % 2, bvs[t - 2])
            if last:
                for rnd in range(3):
                    j0 = rnd * 8
                    nj = min(8, NJF - j0)
                    bp = alloc_pair(0, 4)
                    for jj in range(nj):
                        P.op("pe", lambda e, jj=jj, j0=j0, bp=bp: e.transpose(
                            ps[0:34, 2 * bp + jj // 4, (jj % 4) * 128:(jj % 4 + 1) * 128], fst[:, j0 + jj, :], ident_f[:]),
                            r=["fst:%d" % (j0 + jj), "ident_f"], w=[PS(2 * bp), PS(2 * bp + 1)])
                    s_ = state["xs"]
                    state["xs"] ^= 1
                    P.op("act", lambda e, bp=bp, s_=s_, nj=nj: e.activation(
                        out=xs[s_][0:34, 0:nj * 128], in_=ps[0:34, 2 * bp:2 * bp + 2, :].rearrange("p a n -> p (a n)")[:, 0:nj * 128],
                        func=AF.Copy), r=[PS(2 * bp), PS(2 * bp + 1)], w=["xs%d" % s_])
                    P.op("sp", lambda e, s_=s_, j0=j0, nj=nj: e.dma_start(out=o_ffn_s[:, j0 * 128:(j0 + nj) * 128], in_=xs[s_][0:32, 0:nj * 128]),
                         r=["xs%d" % s_], dma="xs%d" % s_)
                    P.op("sp", lambda e, s_=s_, j0=j0, nj=nj: e.dma_start(out=o_ffn_p[:, j0 * 128:(j0 + nj) * 128], in_=xs[s_][32:34, 0:nj * 128]),
                         r=["xs%d" % s_], dma="xs%d" % s_)

            ckpt("c%dJ" % ci)
            def final_pieces(gi):
                g0, n, kind = groups[gi]
                base_ = ch["base"]
                xk_ = ["xT:%d:%d" % (kc, gi) for kc in range(8)]
                stb = {}
                out = []

                def sq(kc):
                    sl_ = kc % 2
                    P.op("act", lambda e, kc=kc, sl_=sl_: e.activation(out=pT[sl_][:, 0:n], in_=xT[:, kc, g0:g0 + n], func=AF.Square),
                         r=[xk_[kc]], w=["pT%d" % sl_])

                def p1(kc):
                    if kc == 0:
                        stb["b"] = alloc()
                        sq(0)
                        sq(1)
                    b = stb["b"]
                    sl_ = kc % 2
                    P.op("pe", lambda e, kc=kc, b=b, sl_=sl_: e.matmul(ps[:, b, 0:n], lhsT=ones_b[:], rhs=pT[sl_][:, 0:n],
                                                                      start=(kc == 0), stop=(kc == 7)),
                         r=["ones_b", "pT%d" % sl_], w=[PS(b)])
                    if kc + 2 < 8:
                        sq(kc + 2)

                def p2():
                    b = stb["b"]
                    P.op("act", lambda e, b=b: e.activation(out=rs[:, 0:n], in_=ps[:, b, 0:n], func=AF.Ln, scale=1.0 / D, bias=1e-6),
                         r=[PS(b)], w=["rs"])
                    P.op("act", lambda e: e.activation(out=rstd[:, 0:n], in_=rs[:, 0:n], func=AF.Exp, scale=-0.5), r=["rs"], w=["rstd"])

                def p3(kc):
                    P.op("dve", lambda e, kc=kc: e.scalar_tensor_tensor(
                        out=xT[:, kc, g0:g0 + n], in0=xT[:, kc, g0:g0 + n], scalar=vecA[:, 64 + kc:65 + kc], in1=rstd[:, 0:n],
                        op0=ALU.mult, op1=ALU.mult), r=[xk_[kc], "vecA", "rstd"], w=[xk_[kc]])

                def p4(tl):
                    tok = g0 + tl * 128
                    row0 = TP if kind == "s" else base_ + tok
                    bp = alloc_pair(0, 4)
                    for kc in range(8):
                        P.op("pe", lambda e, kc=kc, bp=bp, tok=tok: e.transpose(
                            ps[:, 2 * bp + kc // 4, (kc % 4) * 128:(kc % 4 + 1) * 128], xT[:, kc, tok:tok + 128], ident_f[:]),
                            r=[xk_[kc], "ident_f"], w=[PS(2 * bp), PS(2 * bp + 1)])
                    s_ = state["xs"]
                    state["xs"] ^= 1
                    P.op("act", lambda e, bp=bp, s_=s_: e.activation(out=xs[s_][:], in_=ps[:, 2 * bp:2 * bp + 2, :].rearrange("p a n -> p (a n)"),
                                                                     func=AF.Copy), r=[PS(2 * bp), PS(2 * bp + 1)], w=["xs%d" % s_])
                    P.op("sp", lambda e, s_=s_, row0=row0: e.dma_start(out=y_d[row0:row0 + 128, :], in_=xs[s_][:]),
                         r=["xs%d" % s_], dma="xs%d" % s_)

                for kc in range(8):
                    out.append(lambda kc=kc: p1(kc))
                out.append(p2)
                for kc in range(8):
                    out.append(lambda kc=kc: p3(kc))
                for tl in range(n // 128):
                    out.append(lambda tl=tl: p4(tl))
                return out

            def final_A(gi):
                pass

            def final_B(gi):
                g0, n, kind = groups[gi]
                xk_ = ["xT:%d:%d" % (kc, gi) for kc in range(8)]
                b = alloc()
                for kc in range(8):
                    sl_ = kc % 2
                    P.op("act", lambda e, kc=kc, g0=g0, n=n, sl_=sl_: e.activation(out=pT[sl_][:, 0:n], in_=xT[:, kc, g0:g0 + n], func=AF.Square),
                         r=[xk_[kc]], w=["pT%d" % sl_])
                    P.op("pe", lambda e, kc=kc, n=n, b=b, sl_=sl_: e.matmul(ps[:, b, 0:n], lhsT=ones_b[:], rhs=pT[sl_][:, 0:n],
                                                                          start=(kc == 0), stop=(kc == 7)),
                         r=["ones_b", "pT%d" % sl_], w=[PS(b)])
                P.op("act", lambda e, n=n, b=b: e.activation(out=rs[:, 0:n], in_=ps[:, b, 0:n], func=AF.Ln, scale=1.0 / D, bias=1e-6),
                     r=[PS(b)], w=["rs"])
                P.op("act", lambda e, n=n: e.activation(out=rstd[:, 0:n], in_=rs[:, 0:n], func=AF.Exp, scale=-0.5), r=["rs"], w=["rstd"])
                for kc in range(8):
                    P.op("dve", lambda e, kc=kc, g0=g0, n=n: e.scalar_tensor_tensor(
                        out=xT[:, kc, g0:g0 + n], in0=xT[:, kc, g0:g0 + n], scalar=vecA[:, 64 + kc:65 + kc], in1=rstd[:, 0:n],
                        op0=ALU.mult, op1=ALU.mult), r=[xk_[kc], "vecA", "rstd"], w=[xk_[kc]])

            def final_C(gi):
                g0, n, kind = groups[gi]
                xk_ = ["xT:%d:%d" % (kc, gi) for kc in range(8)]
                for tl in range(n // 128):
                    tok = g0 + tl * 128
                    row0 = TP if kind == "s" else ch["base"] + tok
                    bp = alloc_pair(0, 4)
                    for kc in range(8):
                        P.op("pe", lambda e, kc=kc, bp=bp, tok=tok: e.transpose(
                            ps[:, 2 * bp + kc // 4, (kc % 4) * 128:(kc % 4 + 1) * 128], xT[:, kc, tok:tok + 128], ident_f[:]),
                            r=[xk_[kc], "ident_f"], w=[PS(2 * bp), PS(2 * bp + 1)])
                    s_ = state["xs"]
                    state["xs"] ^= 1
                    P.op("act", lambda e, bp=bp, s_=s_: e.activation(out=xs[s_][:], in_=ps[:, 2 * bp:2 * bp + 2, :].rearrange("p a n -> p (a n)"),
                                                                     func=AF.Copy), r=[PS(2 * bp), PS(2 * bp + 1)], w=["xs%d" % s_])
                    P.op("sp", lambda e, s_=s_, row0=row0: e.dma_start(out=y_d[row0:row0 + 128, :], in_=xs[s_][:]),
                         r=["xs%d" % s_], dma="xs%d" % s_)

            npre = 0
            if not last:
                nxt = CHUNKS[ci + 1]
                npre = nxt["ntok"] // 128 - (1 if ci + 1 == len(CHUNKS) - 1 else 0)
            for j in range(8):
                if 1 <= j <= npre:
                    pre_norm(ci + 1, j - 1, (j - 1) % 2)
                r_ = ring_load(wdn_t[j], 2816)
                wv_ = ring[r_][:, 0:2816].rearrange("p (k n) -> p k n", k=NJF)
                for gi, (g0, n, kind) in enumerate(groups):
                    b = alloc()
                    for kc in range(NJF):
                        ld_ = wv_[:, kc, :]
                        P.op("pe", lambda e, kc=kc, g0=g0, n=n, b=b, ld_=ld_: e.matmul(ps[:, b, 0:n], lhsT=ld_, rhs=AR[:, kc, g0:g0 + n],
                                                                             start=(kc == 0), stop=(kc == NJF - 1)),
                             r=["ring%d" % r_, ARK(kc, gi)], w=[PS(b)])
                    resid_add(b, n, j, g0, gi, kind, 40)
            ckpt("c%dK" % ci)
            pieces = []
            for gi in range(len(groups)):
                pieces.extend(final_pieces(gi))
            if last:
                for p_ in pieces:
                    p_()
            else:
                nci = ci + 1
                nch = CHUNKS[nci]
                nnt = nch["ntok"] // 128
                nlast = (nci == len(CHUNKS) - 1)
                tiles = ([nnt - 1] if nlast else []) + list(range(nnt - 1 if nlast else nnt))
                prefetch, bsteps = make_B_steps(nci, tiles)
                prefetch()
                fillers.extend(pieces)
                if nlast:
                    fillers.append(bsteps[0])
                    fillers.append(lambda nch=nch: (norm_A(nch, len(nch["groups"]) - 1), norm_B(nch, len(nch["groups"]) - 1, 8, 0)))
                    fillers.extend(bsteps[1:])
                else:
                    fillers.extend(bsteps)

            ckpt("c%dL" % ci)

        P.emit(st)
    return nc


_NC_CACHE = {}


def _prep_shared(inp):
    f = np.float32
    w_in = np.asarray(inp["w_in"][0], f)
    W = w_in.reshape(8, 128, 6656)
    sh = {}
    sh["wada_t"] = np.ascontiguousarray(np.asarray(inp["w_ada"][0], f).reshape(8, 128, 12, 4, 128).transpose(2, 1, 3, 0, 4)).reshape(12, 128, 4096)
    sh["wq_t"] = np.ascontiguousarray(W[:, :, 3072:4096].reshape(8, 128, 2, 4, 128).transpose(2, 1, 3, 0, 4)).reshape(2, 128, 4096)
    wk = W[:, :, 4096:4352].reshape(8, 128, 4, 64).transpose(1, 2, 0, 3)
    sh["wk_t"] = np.ascontiguousarray(np.stack([wk, wk], axis=3)).reshape(128, 4096)
    sh["wv_t"] = np.ascontiguousarray(W[:, :, 4352:4608].transpose(1, 0, 2)).reshape(128, 2048)
    sh["wcv_t"] = np.ascontiguousarray(W[:, :, 0:3072].reshape(8, 128, 3, 8, 128).transpose(3, 1, 0, 2, 4)).reshape(8, 128, 3072)
    comps = np.stack([W[:, :, 4608:5632], W[:, :, 5632:6656],
                      np.asarray(inp["w_conv_out"][0], f).reshape(8, 128, 1024),
                      np.asarray(inp["w_attn_out"][0], f).reshape(8, 128, 1024)], axis=0)
    sh["wmg_t"] = np.ascontiguousarray(comps.reshape(4, 8, 128, 8, 128).transpose(3, 2, 1, 0, 4)).reshape(8, 128, 4096)
    sh["wmix_t"] = np.ascontiguousarray(np.asarray(inp["w_mix_out"][0], f).reshape(8, 128, 2, 4, 128).transpose(2, 1, 3, 0, 4)).reshape(2, 128, 4096)
    sh["wup_t"] = np.ascontiguousarray(np.asarray(inp["w_up"][0], f).reshape(8, 128, 2, NJF, 128).transpose(3, 1, 0, 2, 4)).reshape(NJF, 128, 2048)
    sh["wdn_t"] = np.ascontiguousarray(np.asarray(inp["w_down"][0], f).reshape(NJF, 128, 8, 128).transpose(2, 1, 0, 3)).reshape(8, 128, 2816)
    rowsA = np.zeros((128, 128), f)
    rowsA[0:48] = np.asarray(inp["b_ada"][0], f).reshape(48, 128)
    rowsA[48:56] = np.asarray(inp["norm1_g"][0], f).reshape(8, 128)
    rowsA[56:64] = np.asarray(inp["norm2_g"][0], f).reshape(8, 128)
    rowsA[64:72] = np.asarray(inp["final_g"], f).reshape(8, 128)
    rowsA[72:96] = np.asarray(inp["conv_w"][0], f).reshape(24, 128)
    rowsB = np.zeros((128, 128), f)
    rowsB[0:66] = np.asarray(inp["ffn_conv_w"][0], f).reshape(66, 128)
    sh["rowsA"], sh["rowsB"] = rowsA, rowsB
    sinks = np.asarray(inp["attn_sinks"][0], f)
    p = np.arange(128)[:, None]
    c = np.arange(8)[None, :]
    sh["sinkT"] = np.ascontiguousarray(sinks[2 * c + p // 64]).astype(f)
    sh["ident"] = np.eye(128, dtype=f)
    slopes = np.exp2(-8.0 * np.arange(1, 17, dtype=np.float64) / 16.0)
    k = np.arange(128)[:, None]
    q = np.arange(128)[None, :]
    Dp = np.zeros((128, 4, 2, 2, 2, 128), np.float64)
    for hk in range(4):
        for hh in range(4):
            sl = slopes[4 * hk + hh]
            half, cc = hh % 2, hh // 2
            dist0 = 128 + q - k
            Dp[:, hk, half, 0, cc, :] = np.where(dist0 <= 128, np.exp(-sl * dist0), 0.0)
            dist1 = q - k
            Dp[:, hk, half, 1, cc, :] = np.where(dist1 >= 0, np.exp(-sl * np.maximum(dist1, 0)), 0.0)
    sh["Dp"] = Dp.reshape(128, 4096).astype(f)
    ks, kt = k // 8, k % 8
    qs_, qt = q // 8, q % 8
    Dn = np.zeros((128, 4, 2, 2, 128), np.float64)
    for hk in range(4):
        for hh in range(4):
            sl = slopes[4 * hk + hh]
            Dn[:, hk, hh % 2, hh // 2, :] = np.where((ks == qs_) & (kt <= qt), np.exp(-sl * np.maximum(qt - kt, 0)), 0.0)
    sh["Dn"] = Dn.reshape(128, 2048).astype(f)
    Dc = np.zeros((128, 16, 8), np.float64)
    jj = np.arange(128)[:, None]
    ii = np.arange(8)[None, :]
    for hk in range(4):
        for half in range(2):
            for cc in range(2):
                sl = slopes[4 * hk + 2 * cc + half]
                dist = 128 + ii - jj
                Dc[:, half * 8 + hk * 2 + cc, :] = np.where(jj >= ii, np.exp(-sl * dist), 0.0)
    sh["Dc"] = Dc.reshape(128, 128).astype(f)
    return sh


def kernel(**inp):
    f = np.float32
    if "nc" not in _NC_CACHE:
        _NC_CACHE["nc"] = build_nc()
    nc = _NC_CACHE["nc"]
    sh = _prep_shared(inp)
    xp = np.asarray(inp["x_prompt"], f)
    xsm = np.asarray(inp["x_sample"], f)
    cp = np.asarray(inp["c_prompt"], f)
    csm = np.asarray(inp["c_sample"], f)
    stc = np.asarray(inp["state_conv"][0], f)
    stf = np.asarray(inp["state_ffn_conv"][0], f)
    ckw = np.asarray(inp["cache_k_win"][0], f)
    cvw = np.asarray(inp["cache_v_win"][0], f)
    in_maps = []
    for c in range(NCORES):
        m = dict(sh)
        m["xin"] = np.concatenate([xp[c], xsm[c * 16:(c + 1) * 16].reshape(128, D)], axis=0)
        cv_ = np.zeros((32, D), f)
        cv_[0] = cp[c]
        cv_[1:17] = csm[c * 16:(c + 1) * 16]
        m["cvec"] = cv_
        m["st_conv"] = np.ascontiguousarray(stc[c * 16:(c + 1) * 16].reshape(32, D))
        m["st_ffn"] = np.ascontiguousarray(stf[c * 16:(c + 1) * 16].reshape(32, DFF))
        m["ck"] = np.ascontiguousarray(ckw[c * 16:(c + 1) * 16].reshape(16, 128, 256))
        m["cv"] = np.ascontiguousarray(cvw[c * 16:(c + 1) * 16].reshape(16, 128, 256))
        kt = ckw[c * 16:(c + 1) * 16].reshape(8, 2, 128, 4, 64).transpose(4, 0, 1, 3, 2)
        m["ckT"] = np.ascontiguousarray(np.concatenate([kt, kt], axis=0)).reshape(128, 8, 1024)
        in_maps.append(m)
    res = run_bass_kernel_spmd(nc, in_maps, core_ids=list(range(NCORES)))
    R = res.results
    y_p = np.stack([R[c]["y"][0:TP] for c in range(NCORES)], 0)
    y_s = np.concatenate([R[c]["y"][TP:].reshape(16, 8, D) for c in range(NCORES)], 0)
    conv_p = np.stack([R[c]["o_conv_p"] for c in range(NCORES)], 0)[None]
    k_p = np.stack([R[c]["o_k_p"].reshape(128, 4, 64) for c in range(NCORES)], 0)[None]
    v_p = np.stack([R[c]["o_v_p"].reshape(128, 4, 64) for c in range(NCORES)], 0)[None]
    ffn_p = np.stack([R[c]["o_ffn_p"] for c in range(NCORES)], 0)[None]
    conv_s = np.concatenate([R[c]["o_conv_s"].reshape(16, 2, D) for c in range(NCORES)], 0)[None]
    k_s = np.concatenate([R[c]["o_k_s"].reshape(16, 128, 4, 64) for c in range(NCORES)], 0)[None]
    v_s = np.concatenate([R[c]["o_v_s"].reshape(16, 128, 4, 64) for c in range(NCORES)], 0)[None]
    ffn_s = np.concatenate([R[c]["o_ffn_s"].reshape(16, 2, DFF) for c in range(NCORES)], 0)[None]
    outs = (y_p, y_s, conv_p, k_p, v_p, ffn_p, conv_s, k_s, v_s, ffn_s)
    return tuple(np.ascontiguousarray(o, dtype=f) for o in outs)
```

```python
import numpy as np
from contextlib import ExitStack
import concourse.bass as bass
import concourse.mybir as mybir
from concourse.bass_utils import run_bass_kernel_spmd

F32 = mybir.dt.float32
BF16 = mybir.dt.bfloat16
AF = mybir.ActivationFunctionType
ALU = mybir.AluOpType

NCORES = 8
D = 1024
DFF = 2816
NJF = 22
TP = 2048
TS = 128
TT_ALL = TP + TS
CHUNKS = [
    dict(base=0, ntok=768, groups=[(0, 512, "p"), (512, 256, "p")]),
    dict(base=768, ntok=768, groups=[(0, 512, "p"), (512, 256, "p")]),
    dict(base=1536, ntok=640, groups=[(0, 512, "p"), (512, 128, "s")]),
]
TC = 768
RING_ELEMS = 4096
NRING = 3


class Op:
    __slots__ = ("eng", "fn", "deps", "dma", "signal", "count", "idx")


class Prog:
    ENG = ("pe", "act", "dve", "pool", "sp")

    def __init__(self, nc):
        self.nc = nc
        self.q = {e: [] for e in self.ENG}
        self.lastw = {}
        self.readers = {}
        self.dma_groups = {}
        self.disabled = False

    def op(self, eng, fn, r=(), w=(), dma=None):
        if self.disabled:
            return None
        o = Op()
        o.eng, o.fn, o.dma, o.signal, o.count = eng, fn, dma, False, 0
        deps = set()
        for k in r:
            x = self.lastw.get(k)
            if x is not None:
                deps.add(x)
            if k.startswith("ps"):
                for x in self.readers.get(k, ()):
                    if x.eng != eng:
                        deps.add(x)
        for k in w:
            x = self.lastw.get(k)
            if x is not None:
                deps.add(x)
            for x in self.readers.get(k, ()):
                deps.add(x)
        rset = set(r)
        keep = []
        latest = {}
        for x in deps:
            if x.dma is not None:
                keep.append(x)
                continue
            if x.eng == eng and eng == "pe":
                continue
            y = latest.get(x.eng)
            if y is None or x.idx > y.idx:
                latest[x.eng] = x
        keep.extend(latest.values())
        for x in keep:
            x.signal = True
        o.deps = keep
        o.idx = len(self.q[eng])
        if dma is not None:
            g = self.dma_groups.setdefault(dma, [])
            g.append(o)
            o.count = 16 * len(g)
        wset = set(w)
        for k in wset:
            self.lastw[k] = o
            self.readers[k] = []
        for k in rset:
            if k not in wset:
                self.readers.setdefault(k, []).append(o)
        self.q[eng].append(o)
        return o

    def emit(self, stack):
        nc = self.nc
        sems = {e: stack.enter_context(nc.semaphore("s_" + e)) for e in self.ENG}
        gsem = {g: stack.enter_context(nc.semaphore("d_" + g)) for g in self.dma_groups}
        for e in self.ENG:
            c = 0
            for o in self.q[e]:
                if o.dma is None and o.signal:
                    c += 1
                    o.count = c
        block = stack.enter_context(nc.Block())
        handles = {"pe": block.tensor, "act": block.scalar, "dve": block.vector,
                   "pool": block.gpsimd, "sp": block.sync}
        for e in self.ENG:
            ops = self.q[e]

            def body(eng, ops=ops, e=e):
                seen = {}
                for o in ops:
                    need = {}
                    for x in o.deps:
                        s = gsem[x.dma] if x.dma is not None else sems[x.eng]
                        key = id(s)
                        if x.count > seen.get(key, 0):
                            if key not in need or need[key][1] < x.count:
                                need[key] = (s, x.count)
                    for key, (s, cnt) in need.items():
                        eng.wait_ge(s, cnt)
                        seen[key] = cnt
                    ins = o.fn(eng)
                    if o.dma is not None:
                        ins.then_inc(gsem[o.dma], 16)
                    elif o.signal:
                        ins.then_inc(sems[e], 1)
                if e == "sp":
                    for g, lst in self.dma_groups.items():
                        eng.wait_ge(gsem[g], 16 * len(lst))
            handles[e](body)


def build_nc():
    nc = bass.Bass("TRN2", target_bir_lowering=False)

    def din(name, shape):
        return nc.dram_tensor(name, list(shape), F32, kind="ExternalInput").ap()

    def dout(name, shape):
        return nc.dram_tensor(name, list(shape), F32, kind="ExternalOutput").ap()

    xin = din("xin", [TT_ALL, D])
    cvec = din("cvec", [32, D])
    st_conv = din("st_conv", [32, D])
    st_ffn = din("st_ffn", [32, DFF])
    ck = din("ck", [16, 128, 256])
    cv = din("cv", [16, 128, 256])
    ckT_d = din("ckT", [128, 8, 1024])
    rowsA = din("rowsA", [128, 128])
    rowsB = din("rowsB", [128, 128])
    sinkT_d = din("sinkT", [128, 8])
    ident_d = din("ident", [128, 128])
    Dp_d = din("Dp", [128, 4096])
    Dn_d = din("Dn", [128, 2048])
    Dc_d = din("Dc", [128, 128])
    wada_t = din("wada_t", [12, 128, 4096])
    wq_t = din("wq_t", [2, 128, 4096])
    wk_t = din("wk_t", [128, 4096])
    wv_t = din("wv_t", [128, 2048])
    wcv_t = din("wcv_t", [8, 128, 3072])
    wmg_t = din("wmg_t", [8, 128, 4096])
    wmix_t = din("wmix_t", [2, 128, 4096])
    wup_t = din("wup_t", [NJF, 128, 2048])
    wdn_t = din("wdn_t", [8, 128, 2816])

    y_d = dout("y", [TT_ALL, D])
    o_conv_p = dout("o_conv_p", [2, D])
    o_k_p = dout("o_k_p", [128, 256])
    o_v_p = dout("o_v_p", [128, 256])
    o_ffn_p = dout("o_ffn_p", [2, DFF])
    o_conv_s = dout("o_conv_s", [32, D])
    o_k_s = dout("o_k_s", [16, 128, 256])
    o_v_s = dout("o_v_s", [16, 128, 256])
    o_ffn_s = dout("o_ffn_s", [32, DFF])

    with ExitStack() as st:
        def sb(name, shape, dt):
            return st.enter_context(nc.sbuf_tensor(name, list(shape), dt))

        xT = sb("xT", [128, 8, TC], F32)
        hT = sb("hT", [128, 8, TC], BF16)
        AR = sb("arena", [128, 24, TC], BF16)
        kT = sb("kT", [128, 4, 128 + TC], BF16)
        vsb = sb("vsb", [128, 7, 256], BF16)
        ring = [sb("ring%d" % i, [128, RING_ELEMS], BF16) for i in range(NRING)]
        Dp = sb("Dp_sb", [128, 4096], BF16)
        Dn = sb("Dn_sb", [128, 2048], BF16)
        Dc = sb("Dc_sb", [128, 128], BF16)
        modT = sb("modT", [128, 48, 32], F32)
        xs = [sb("xs%d" % i, [128, 1024], F32) for i in range(2)]
        xl = [sb("xl%d" % i, [128, 1024], F32) for i in range(2)]
        pT = [sb("pT%d" % i, [128, 1024], BF16) for i in range(2)]
        C_sb = [sb("C_sb%d" % i, [128, 512], F32) for i in range(2)]
        B_sb = [sb("B_sb%d" % i, [128, 512], BF16) for i in range(2)]
        uext = [sb("uext%d" % i, [128, 2 + TC], BF16) for i in range(3)]
        ues = sb("ues", [128, 8, 16, 10], BF16)
        aes = sb("aes", [128, NJF, 16, 10], BF16)
        uhalo = sb("uhalo", [128, 8, 2], BF16)
        ahalo = sb("ahalo", [128, NJF, 2], BF16)
        sga = [sb("sga%d" % i, [128, 512], BF16) for i in range(2)]
        sgb = [sb("sgb%d" % i, [128, 512], BF16) for i in range(2)]
        t1s = [sb("t1_%d" % i, [128, 512], BF16) for i in range(2)]
        t2s = [sb("t2_%d" % i, [128, 512], BF16) for i in range(2)]
        ge = [sb("ge%d" % i, [128, 512], BF16) for i in range(2)]
        rs = sb("rs", [128, 512], F32)
        rstd = sb("rstd", [128, 512], F32)
        tmp = [sb("tmp%d" % i, [128, 512], F32) for i in range(2)]
        rden = sb("rden", [128, 1024], F32)
        xn = rden.bitcast(BF16)[:, 0:1024]
        stok = sb("stok", [128, 32], F32)
        dgc = [sb("dgc%d" % i, [128, 3, 128], BF16) for i in range(2)]
        cst = sb("cst", [128, 8, 34], F32)
        fst = sb("fst", [128, NJF, 34], F32)
        kstage = [sb("kstage%d" % i, [128, 256], F32) for i in range(2)]
        vstage = [sb("vstage%d" % i, [128, 256], F32) for i in range(2)]
        kcd = sb("kcd", [128, 2, 4, 128], BF16)
        kcT = sb("kcT", [128, 2, 4, 128], BF16)
        ident_f = sb("ident_f", [128, 128], F32)
        ident_b = sb("ident_b", [128, 128], BF16)
        ones_b = sb("ones_b", [128, 128], BF16)
        vecA = sb("vecA", [128, 128], F32)
        vecB = sb("vecB", [128, 128], F32)
        esink = sb("esink", [128, 8], F32)
        scT = sb("scT", [128, 8, 32], BF16)
        xT_flat = xT[:, :, :].rearrange("p a t -> p (a t)")
        hT_flat = hT[:, :, :].rearrange("p a t -> p (a t)")
        stf_sb = xT_flat[0:32, 0:2816]
        stc_sb = xT_flat[0:32, 2816:3840]
        cs_f = xT_flat[0:32, 3840:4864]
        rowsA_sb = xT_flat[:, 4864:4992]
        rowsB_sb = xT_flat[:, 4992:5120]
        cs_b = hT_flat[0:32, 0:1024]
        XT_ALL = ["xT:%d:%d" % (kc, g) for kc in range(8) for g in range(2)]
        HT_ALL = ["hT:%d:%d" % (kc, g) for kc in range(8) for g in range(2)]
        ps = st.enter_context(nc.psum_tensor("ps", [128, 8, 512], F32))
        ps_bf = ps.bitcast(BF16)

        ar_flat = AR[:, 16:24, :].rearrange("p a t -> p (a t)")
        vc = ar_flat[:, 0:4096].rearrange("p (s d) -> p s d", d=256)
        pTc = ar_flat[:, 4096:6144].rearrange("p (s c) -> p s c", c=128)
        pTn_t = sb("pTn", [128, 4, 512], BF16)

        P = Prog(nc)
        state = dict(bank=0, pair=0, ring=0, xs=0)
        import os
        KSTOP = os.environ.get("KSTOP", "")

        def ckpt(name):
            if KSTOP == name:
                P.disabled = True

        def alloc():
            b = state["bank"]
            state["bank"] = (b + 1) % 8
            return b

        def alloc_pair(lo=0, n=4):
            p = state["pair"] % n
            state["pair"] = (p + 1) % n
            return lo + p

        def PS(b):
            return "ps%d" % b

        def ring_load(src_ap, nel):
            r_ = state["ring"]
            state["ring"] = (r_ + 1) % NRING
            P.op("pool", lambda e, r_=r_: e.dma_start(out=ring[r_][:, 0:nel], in_=src_ap),
                 w=["ring%d" % r_], dma="ring%d" % r_)
            return r_

        def ARK(idx, gi):
            return "AR:%d:%d" % (idx, gi)

        GT_ALL = [ARK(i, g) for i in range(16, 24) for g in range(2)]

        P.op("sp", lambda e: e.dma_start(out=ident_f[:], in_=ident_d), w=["ident_f"], dma="c0")
        P.op("sp", lambda e: e.dma_start(out=rowsA_sb, in_=rowsA), w=["rowsA"], dma="c1")
        P.op("sp", lambda e: e.dma_start(out=rowsB_sb, in_=rowsB), w=["rowsB"], dma="c2")
        P.op("sp", lambda e: e.dma_start(out=esink[:], in_=sinkT_d), w=["esink"], dma="c3")
        P.op("sp", lambda e: e.dma_start(out=cs_f, in_=cvec), w=["cs_f"], dma="c4")
        P.op("sp", lambda e: e.dma_start(out=stc_sb, in_=st_conv), w=["stc"], dma="c5")
        P.op("sp", lambda e: e.dma_start(out=stf_sb, in_=st_ffn), w=["stf"], dma="c6")
        P.op("dve", lambda e: e.tensor_copy(out=ident_b[:], in_=ident_f[:]), r=["ident_f"], w=["ident_b"])
        P.op("dve", lambda e: e.memset(ones_b[:], 1.0), w=["ones_b"])
        P.op("dve", lambda e: e.memset(uext[0][:, 0:2], 0.0), w=["uh0"])
        P.op("dve", lambda e: e.memset(uext[1][:, 0:2], 0.0), w=["uh1"])
        P.op("act", lambda e: e.activation(out=esink[:], in_=esink[:], func=AF.Exp), r=["esink"], w=["esink"])
        b0 = alloc()
        P.op("pe", lambda e: e.transpose(ps[:, b0, 0:128], rowsA_sb, ident_f[:]), r=["rowsA", "ident_f"] + XT_ALL, w=[PS(b0)])
        P.op("dve", lambda e: e.tensor_copy(out=vecA[:], in_=ps[:, b0, 0:128]), r=[PS(b0)], w=["vecA"])
        b1 = alloc()
        P.op("pe", lambda e: e.transpose(ps[:, b1, 0:128], rowsB_sb, ident_f[:]), r=["rowsB", "ident_f"] + XT_ALL, w=[PS(b1)])
        P.op("dve", lambda e: e.tensor_copy(out=vecB[:], in_=ps[:, b1, 0:128]), r=[PS(b1)], w=["vecB"])
        bp = alloc_pair()
        for j in range(8):
            P.op("pe", lambda e, j=j, bp=bp: e.transpose(ps[:, 2 * bp + j // 4, (j % 4) * 128:(j % 4) * 128 + 32],
                                                  stc_sb[:, j * 128:(j + 1) * 128], ident_f[0:32, 0:32]),
                 r=["stc", "ident_f"] + XT_ALL, w=[PS(2 * bp), PS(2 * bp + 1)])
        P.op("dve", lambda e, bp=bp: e.tensor_copy(
            out=ues[:, :, :, 0:2],
            in_=ps[:, 2 * bp:2 * bp + 2, :].rearrange("p a (j c) -> p (a j) c", c=128)[:, :, 0:32].rearrange("p j (s r) -> p j s r", r=2)),
            r=[PS(2 * bp), PS(2 * bp + 1)], w=["ues_h"])
        for rnd in range(3):
            bp = alloc_pair()
            j0 = rnd * 8
            nj = min(8, NJF - j0)
            for jj in range(nj):
                P.op("pe", lambda e, jj=jj, j0=j0, bp=bp: e.transpose(
                    ps[:, 2 * bp + jj // 4, (jj % 4) * 128:(jj % 4) * 128 + 32],
                    stf_sb[:, (j0 + jj) * 128:(j0 + jj + 1) * 128], ident_f[0:32, 0:32]),
                    r=["stf", "ident_f"] + XT_ALL, w=[PS(2 * bp), PS(2 * bp + 1)])
            P.op("dve", lambda e, j0=j0, nj=nj, bp=bp: e.tensor_copy(
                out=aes[:, j0:j0 + nj, :, 0:2],
                in_=ps[:, 2 * bp:2 * bp + 2, :].rearrange("p a (j c) -> p (a j) c", c=128)[:, 0:nj, 0:32].rearrange("p j (s r) -> p j s r", r=2)),
                r=[PS(2 * bp), PS(2 * bp + 1)], w=["aes_h"])
        P.op("act", lambda e: e.activation(out=cs_b, in_=cs_f, func=AF.Silu), r=["cs_f"] + XT_ALL, w=["cs_b"] + HT_ALL)
        b2 = alloc()
        for kc in range(8):
            P.op("pe", lambda e, kc=kc: e.transpose(ps_bf[:, b2, kc * 32:(kc + 1) * 32], cs_b[:, kc * 128:(kc + 1) * 128],
                                                    ident_b[0:32, 0:32]),
                 r=["cs_b", "ident_b"] + HT_ALL, w=[PS(b2)])
        P.op("dve", lambda e: e.tensor_copy(out=scT[:].rearrange("p k c -> p (k c)"), in_=ps_bf[:, b2, 0:256]), r=[PS(b2)], w=["scT"])
        ckpt("setup")

        def sview(ap2d):
            return ap2d.rearrange("p (s t) -> p s t", t=8)

        def mod_b(idx):
            return modT[:, idx, 1:17].unsqueeze(2).to_broadcast([128, 16, 8])

        def norm_A(ch, gi):
            g0, n, kind = ch["groups"][gi]
            hk_ = ["hT:%d:%d" % (kc, gi) for kc in range(8)]
            xk_ = ["xT:%d:%d" % (kc, gi) for kc in range(8)]
            P.op("act", lambda e, g0=g0, n=n: e.activation(out=hT[:, :, g0:g0 + n], in_=xT[:, :, g0:g0 + n], func=AF.Square),
                 r=xk_, w=hk_)

        def norm_stats(ch, gi):
            g0, n, kind = ch["groups"][gi]
            hk_ = ["hT:%d:%d" % (kc, gi) for kc in range(8)]
            b = alloc()
            for kc in range(8):
                P.op("pe", lambda e, kc=kc, g0=g0, n=n, b=b: e.matmul(ps[:, b, 0:n], lhsT=ones_b[:], rhs=hT[:, kc, g0:g0 + n],
                                                                     start=(kc == 0), stop=(kc == 7)),
                     r=["ones_b", hk_[kc]], w=[PS(b)])
            P.op("act", lambda e, n=n, b=b: e.activation(out=rs[:, 0:n], in_=ps[:, b, 0:n], func=AF.Ln, scale=1.0 / D, bias=1e-6),
                 r=[PS(b)], w=["rs"])
            P.op("act", lambda e, n=n: e.activation(out=rstd[:, 0:n], in_=rs[:, 0:n], func=AF.Exp, scale=-0.5), r=["rs"], w=["rstd"])

        def norm_B(ch, gi, s_idx, sh_idx):
            g0, n, kind = ch["groups"][gi]
            hk_ = ["hT:%d:%d" % (kc, gi) for kc in range(8)]
            xk_ = ["xT:%d:%d" % (kc, gi) for kc in range(8)]
            norm_stats(ch, gi)
            for kc in range(8):
                tb = kc % 2
                P.op("dve", lambda e, kc=kc, g0=g0, n=n, tb=tb: e.tensor_tensor(
                    out=tmp[tb][:, 0:n], in0=xT[:, kc, g0:g0 + n], in1=rstd[:, 0:n], op=ALU.mult),
                    r=[xk_[kc], "rstd"], w=["tmp%d" % tb])
                if kind == "p":
                    P.op("act", lambda e, kc=kc, g0=g0, n=n, tb=tb: e.activation(
                        out=hT[:, kc, g0:g0 + n], in_=tmp[tb][:, 0:n], func=AF.Identity,
                        scale=modT[:, s_idx + kc, 0:1], bias=modT[:, sh_idx + kc, 0:1]),
                        r=["tmp%d" % tb, MK(s_idx), MK(sh_idx)], w=[hk_[kc]])
                else:
                    P.op("dve", lambda e, kc=kc, n=n, tb=tb: e.tensor_tensor(
                        out=sview(tmp[tb][:, 0:n]), in0=sview(tmp[tb][:, 0:n]), in1=mod_b(s_idx + kc), op=ALU.mult),
                        r=["tmp%d" % tb, MK(s_idx)], w=["tmp%d" % tb])
                    P.op("dve", lambda e, kc=kc, g0=g0, n=n, tb=tb: e.tensor_tensor(
                        out=sview(hT[:, kc, g0:g0 + n]), in0=sview(tmp[tb][:, 0:n]), in1=mod_b(sh_idx + kc), op=ALU.add),
                        r=["tmp%d" % tb, MK(sh_idx)], w=[hk_[kc]])

        def norm_mod(ci, ch, s_idx, sh_idx):
            for gi in range(len(ch["groups"])):
                norm_A(ch, gi)
                norm_B(ch, gi, s_idx, sh_idx)

        def resid_add(b, n, j, g0, gi, kind, g_idx):
            xk = "xT:%d:%d" % (j, gi)
            if kind == "p":
                P.op("dve", lambda e: e.scalar_tensor_tensor(
                    out=xT[:, j, g0:g0 + n], in0=ps[:, b, 0:n], scalar=modT[:, g_idx + j, 0:1], in1=xT[:, j, g0:g0 + n],
                    op0=ALU.mult, op1=ALU.add), r=[PS(b), MK(g_idx), xk], w=[xk])
            else:
                P.op("dve", lambda e: e.tensor_tensor(out=sview(tmp[0][:, 0:n]), in0=sview(ps[:, b, 0:n]), in1=mod_b(g_idx + j),
                                                      op=ALU.mult), r=[PS(b), MK(g_idx)], w=["tmp0"])
                P.op("dve", lambda e: e.tensor_tensor(out=xT[:, j, g0:g0 + n], in0=tmp[0][:, 0:n], in1=xT[:, j, g0:g0 + n],
                                                      op=ALU.add), r=["tmp0", xk], w=[xk])

        def proj8(b, n, wfn, rfn, wkeys, rkeys):
            for kc in range(8):
                l_, r2_ = wfn(kc), rfn(kc)
                P.op("pe", lambda e, kc=kc, l_=l_, r2_=r2_, b=b, n=n: e.matmul(ps[:, b, 0:n], lhsT=l_, rhs=r2_, start=(kc == 0), stop=(kc == 7)),
                     r=[wkeys, rkeys[kc]], w=[PS(b)])

        evac_tog = [0]

        def evac_copy(out_ap, in_ap, r, w):
            evac_tog[0] ^= 1
            if evac_tog[0]:
                P.op("act", lambda e: e.activation(out=out_ap, in_=in_ap, func=AF.Copy), r=r, w=w)
            else:
                P.op("dve", lambda e: e.tensor_copy(out=out_ap, in_=in_ap), r=r, w=w)

        def MK(idx):
            return "modT%d" % (idx // 16)

        def ada_unit(quarter, uu, bank=None):
            bq = alloc() if bank is None else bank
            u = quarter * 4 + uu
            r_ = ring_load(wada_t[u], 4096)
            wv_ = ring[r_][:, :].rearrange("p (c k n) -> p c k n", c=4, k=8)
            for cc in range(4):
                for kc in range(8):
                    P.op("pe", lambda e, wv_=wv_, cc=cc, kc=kc, bq=bq: e.matmul(
                        ps[:, bq, cc * 32:(cc + 1) * 32], lhsT=wv_[:, cc, kc, :], rhs=scT[:, kc, :],
                        start=(kc == 0), stop=(kc == 7)),
                        r=["ring%d" % r_, "scT"], w=[PS(bq)])
            c0 = quarter * 16 + uu * 4
            P.op("dve", lambda e, c0=c0, bq=bq: e.tensor_tensor(
                out=modT[:, c0:c0 + 4, :],
                in0=ps[:, bq, 0:128].rearrange("p (c s) -> p c s", s=32),
                in1=vecA[:, c0:c0 + 4].unsqueeze(2).to_broadcast([128, 4, 32]),
                op=ALU.add), r=[PS(bq), "vecA"], w=["modT%d" % quarter])

        def ada_quarter(quarter):
            for uu in range(4):
                ada_unit(quarter, uu)

        def mod_scale(c0, gcol):
            P.op("dve", lambda e, c0=c0: e.tensor_scalar(out=modT[:, c0:c0 + 8, :], in0=modT[:, c0:c0 + 8, :], scalar1=1.0,
                                                         scalar2=None, op0=ALU.add), r=[MK(c0)], w=[MK(c0)])
            P.op("dve", lambda e, c0=c0, gcol=gcol: e.tensor_tensor(
                out=modT[:, c0:c0 + 8, :], in0=modT[:, c0:c0 + 8, :],
                in1=vecA[:, gcol:gcol + 8].unsqueeze(2).to_broadcast([128, 8, 32]), op=ALU.mult),
                r=[MK(c0), "vecA"], w=[MK(c0)])

        def tile_row0(ci, tt):
            ch = CHUNKS[ci]
            ntiles = ch["ntok"] // 128
            lastc = (ci == len(CHUNKS) - 1)
            return TP if (lastc and tt == ntiles - 1) else ch["base"] + tt * 128

        def xl_load(ci, tt, s_):
            row0 = tile_row0(ci, tt)
            P.op("sp", lambda e, s_=s_, row0=row0: e.dma_start(out=xl[s_][:], in_=xin[row0:row0 + 128, :]),
                 w=["xl%d" % s_], dma="xl%d" % s_)

        def phase_B(ci, ch, tiles=None):
            groups = ch["groups"]
            ntiles = ch["ntok"] // 128
            tiles = list(range(ntiles)) if tiles is None else list(tiles)
            for i in range(min(2, len(tiles))):
                xl_load(ci, tiles[i], i % 2)
            for i, tt in enumerate(tiles):
                s_ = i % 2
                bp = alloc_pair()
                for kc in range(8):
                    P.op("pe", lambda e, kc=kc, bp=bp, s_=s_: e.transpose(
                        ps[:, 2 * bp + kc // 4, (kc % 4) * 128:(kc % 4 + 1) * 128], xl[s_][:, kc * 128:(kc + 1) * 128], ident_f[:]),
                        r=["xl%d" % s_, "ident_f"], w=[PS(2 * bp), PS(2 * bp + 1)])
                if i + 2 < len(tiles):
                    xl_load(ci, tiles[i + 2], s_)
                gi = [k_ for k_, (g0, n, k) in enumerate(groups) if g0 <= tt * 128 < g0 + n][0]
                evac_copy(xT[:, :, tt * 128:(tt + 1) * 128],
                          ps[:, 2 * bp:2 * bp + 2, :].rearrange("p a (k c) -> p (a k) c", c=128),
                          [PS(2 * bp), PS(2 * bp + 1)], ["xT:%d:%d" % (kc, gi) for kc in range(8)])

        def pre_norm(ci, tt, s_, defer_evac=False):
            groups = CHUNKS[ci]["groups"]
            gi = [k_ for k_, (g0, n, k) in enumerate(groups) if g0 <= tt * 128 < g0 + n][0]
            c0 = 3 * (tt % 8)
            xl_load(ci, tt, s_)
            P.op("act", lambda e, s_=s_, c0=c0: e.activation(out=xn, in_=xl[s_][:], func=AF.Square, accum_out=stok[:, c0:c0 + 1]),
                 r=["xl%d" % s_], w=["rden", "stok"])
            P.op("act", lambda e, c0=c0: e.activation(out=stok[:, c0 + 1:c0 + 2], in_=stok[:, c0:c0 + 1], func=AF.Ln, scale=1.0 / D, bias=1e-6),
                 r=["stok"], w=["stok"])
            P.op("act", lambda e, c0=c0: e.activation(out=stok[:, c0 + 2:c0 + 3], in_=stok[:, c0 + 1:c0 + 2], func=AF.Exp, scale=-0.5),
                 r=["stok"], w=["stok"])
            P.op("act", lambda e, s_=s_, c0=c0: e.activation(out=xn, in_=xl[s_][:], func=AF.Identity, scale=stok[:, c0 + 2:c0 + 3]),
                 r=["xl%d" % s_, "stok"], w=["rden"])
            b = alloc()
            for kc in range(8):
                P.op("pe", lambda e, kc=kc, b=b: e.transpose(ps_bf[:, b, kc * 128:(kc + 1) * 128], xn[:, kc * 128:(kc + 1) * 128], ident_b[:]),
                     r=["rden", "ident_b"], w=[PS(b)])
            if defer_evac:
                return lambda: pre_norm_evac(ci, tt, gi, b)
            pre_norm_evac(ci, tt, gi, b)

        def pre_norm_evac(ci, tt, gi, b):
            for kc in range(8):
                outv = hT[:, kc, tt * 128:(tt + 1) * 128]
                inv = ps_bf[:, b, kc * 128:(kc + 1) * 128]
                if kc % 2 == 0:
                    P.op("act", lambda e, kc=kc, outv=outv, inv=inv: e.activation(
                        out=outv, in_=inv, func=AF.Identity, scale=modT[:, 8 + kc, 0:1], bias=modT[:, kc, 0:1]),
                        r=[PS(b), MK(8), MK(0)], w=["hT:%d:%d" % (kc, gi)])
                else:
                    P.op("dve", lambda e, kc=kc, outv=outv, inv=inv: e.tensor_scalar(
                        out=outv, in0=inv, scalar1=modT[:, 8 + kc, 0:1], scalar2=modT[:, kc, 0:1], op0=ALU.mult, op1=ALU.add),
                        r=[PS(b), MK(8), MK(0)], w=["hT:%d:%d" % (kc, gi)])

        fillers = []

        def fill(k):
            for _ in range(k):
                if fillers:
                    fillers.pop(0)()

        def make_B_steps(ci, tiles):
            ch = CHUNKS[ci]
            groups = ch["groups"]
            st_ = dict(nl=0)

            def issue_load():
                i = st_["nl"]
                if i < len(tiles):
                    xl_load(ci, tiles[i], i % 2)
                    st_["nl"] = i + 1

            def step(i):
                tt = tiles[i]
                s_ = i % 2
                while st_["nl"] <= i:
                    issue_load()
                bp = alloc_pair()
                for kc in range(8):
                    P.op("pe", lambda e, kc=kc, bp=bp, s_=s_: e.transpose(
                        ps[:, 2 * bp + kc // 4, (kc % 4) * 128:(kc % 4 + 1) * 128], xl[s_][:, kc * 128:(kc + 1) * 128], ident_f[:]),
                        r=["xl%d" % s_, "ident_f"], w=[PS(2 * bp), PS(2 * bp + 1)])
                if st_["nl"] <= i + 2:
                    issue_load()
                gi = [k_ for k_, (g0, n, k) in enumerate(groups) if g0 <= tt * 128 < g0 + n][0]
                evac_copy(xT[:, :, tt * 128:(tt + 1) * 128],
                          ps[:, 2 * bp:2 * bp + 2, :].rearrange("p a (k c) -> p (a k) c", c=128),
                          [PS(2 * bp), PS(2 * bp + 1)], ["xT:%d:%d" % (kc, gi) for kc in range(8)])

            def prefetch():
                issue_load()
                issue_load()

            return prefetch, [(lambda i=i: step(i)) for i in range(len(tiles))]

        ada_unit(0, 0)
        phase_B(0, CHUNKS[0])
        for uu in range(1, 4):
            ada_unit(0, uu)
        mod_scale(8, 48)
        norm_mod(0, CHUNKS[0], 8, 0)
        P.op("pool", lambda e: e.dma_start(out=Dp[:], in_=Dp_d), w=["Dp"], dma="m0")
        P.op("pool", lambda e: e.dma_start(out=Dn[:], in_=Dn_d), w=["Dn"], dma="m1")
        P.op("pool", lambda e: e.dma_start(out=Dc[:], in_=Dc_d), w=["Dc"], dma="m2")
        P.op("sp", lambda e: e.dma_start(out=o_k_s[:, 0:120, :], in_=ck[:, 8:128, :]), dma="okc")
        P.op("sp", lambda e: e.dma_start(out=o_v_s[:, 0:120, :], in_=cv[:, 8:128, :]), dma="ovc")

        for ci, ch in enumerate(CHUNKS):
            groups = ch["groups"]
            ntiles = ch["ntok"] // 128
            last = (ci == len(CHUNKS) - 1)

            ckpt("c%dB" % ci)

            HK = lambda kc, gi: "hT:%d:%d" % (kc, gi)

            ckpt("c%dC" % ci)
            wq_v = []
            for u in range(2):
                r_ = ring_load(wq_t[u], 4096)
                wq_v.append((r_, ring[r_][:, :].rearrange("p (c k n) -> p c k n", c=4, k=8)))
            r_k = ring_load(wk_t, 4096)
            wkv = ring[r_k][:, :].rearrange("p (c k n) -> p c k n", c=4, k=8)
            r_ = r_k

            def q_unit(j, gi):
                g0, n, kind = groups[gi]
                rq_, wv_ = wq_v[j // 4]
                jj = j % 4
                b = alloc()
                proj8(b, n, lambda kc, wv_=wv_, jj=jj: wv_[:, jj, kc, :], lambda kc, g0=g0, n=n: hT[:, kc, g0:g0 + n],
                      "ring%d" % rq_, [HK(kc, gi) for kc in range(8)])
                evac_copy(AR[:, j, g0:g0 + n], ps[:, b, 0:n], [PS(b)], [ARK(j, gi)])

            def k_unit(hk, gi):
                g0, n, kind = groups[gi]
                b = alloc()
                proj8(b, n, lambda kc, hk=hk: wkv[:, hk, kc, :], lambda kc, g0=g0, n=n: hT[:, kc, g0:g0 + n],
                      "ring%d" % r_k, [HK(kc, gi) for kc in range(8)])
                evac_copy(kT[:, hk, 128 + g0:128 + g0 + n], ps[:, b, 0:n], [PS(b)], ["kT:%d:%d" % (hk, gi)])

            if last:
                nf = max(1, (len(fillers) + 11) // 12)
                for j in range(8):
                    q_unit(j, 0)
                    fill(nf)
                for hk in range(4):
                    k_unit(hk, 0)
                    fill(nf)
                nrest = ntiles - 1
                fill(max(0, len(fillers) - nrest))
                for j in range(8):
                    q_unit(j, 1)
                    fill(1)
                for hk in range(4):
                    k_unit(hk, 1)
                fill(len(fillers))
            else:
                nf = max(1, (len(fillers) + 23) // 24)
                for j in range(8):
                    for gi in range(len(groups)):
                        q_unit(j, gi)
                        fill(nf)
                for hk in range(4):
                    for gi in range(len(groups)):
                        k_unit(hk, gi)
                        fill(nf)
                fill(len(fillers))
            if last and "nokwin" not in os.environ.get("KDBG", ""):
                for wi, tt in enumerate((ntiles - 2, ntiles - 1)):
                    gi = len(groups) - 2 + wi
                    b = alloc()
                    for hk in range(4):
                        for kc in range(8):
                            rk_ = wkv[:, hk, kc, 0:64]
                            P.op("pe", lambda e, hk=hk, kc=kc, tt=tt, b=b, rk_=rk_: e.matmul(
                                ps[:, b, hk * 64:(hk + 1) * 64], lhsT=hT[:, kc, tt * 128:(tt + 1) * 128], rhs=rk_,
                                start=(kc == 0), stop=(kc == 7)), r=["ring%d" % r_, HK(kc, gi)], w=[PS(b)])
                    P.op("act", lambda e, wi=wi, b=b: e.activation(out=kstage[wi][:], in_=ps[:, b, 0:256], func=AF.Copy),
                         r=[PS(b)], w=["kstage%d" % wi])
                P.op("sp", lambda e: e.dma_start(out=o_k_p, in_=kstage[0][:]), r=["kstage0"], dma="ok")
                for s in range(16 if "nosmall" not in os.environ.get("KDBG", "") else 0):
                    P.op("sp", lambda e, s=s: e.dma_start(out=o_k_s[s, 120:128, :], in_=kstage[1][s * 8:(s + 1) * 8, :]),
                         r=["kstage1"], dma="ok")
            r_ = ring_load(wv_t, 2048)
            wvv = ring[r_][:, 0:2048].rearrange("p (k n) -> p k n", k=8)
            for tt in range(ntiles):
                gi = [i for i, (g0, n, k) in enumerate(groups) if g0 <= tt * 128 < g0 + n][0]
                b = alloc()
                for kc in range(8):
                    rv_ = wvv[:, kc, :]
                    P.op("pe", lambda e, kc=kc, tt=tt, b=b, rv_=rv_: e.matmul(ps[:, b, 0:256], lhsT=hT[:, kc, tt * 128:(tt + 1) * 128],
                                                                    rhs=rv_, start=(kc == 0), stop=(kc == 7)),
                         r=["ring%d" % r_, HK(kc, gi)], w=[PS(b)])
                P.op("dve", lambda e, tt=tt, b=b: e.tensor_copy(out=vsb[:, 1 + tt, :], in_=ps[:, b, 0:256]), r=[PS(b)], w=["v:%d" % (1 + tt)])
                if last and tt >= ntiles - 2 and "novwin" not in os.environ.get("KDBG", ""):
                    wi = tt - (ntiles - 2)
                    P.op("act", lambda e, wi=wi, b=b: e.activation(out=vstage[wi][:], in_=ps[:, b, 0:256], func=AF.Copy),
                         r=[PS(b)], w=["vstage%d" % wi])
                    if wi == 0:
                        P.op("sp", lambda e: e.dma_start(out=o_v_p, in_=vstage[0][:]), r=["vstage0"], dma="ov")
                    else:
                        for s in range(16 if "nosmall" not in os.environ.get("KDBG", "") else 0):
                            P.op("sp", lambda e, s=s: e.dma_start(out=o_v_s[s, 120:128, :], in_=vstage[1][s * 8:(s + 1) * 8, :]),
                                 r=["vstage1"], dma="ov")

            ckpt("c%dD" % ci)
            def gi_of(tok):
                return [i for i, (g0, n, k) in enumerate(groups) if g0 <= tok < g0 + n][0]

            nqb = ntiles - 1 if last else ntiles
            NUMP, DENP = 2, 3
            numv = ps[:, 4:6, :].rearrange("p a (c q) -> p (a c) q", q=128)
            denv = ps[:, 6:8, :].rearrange("p a (c q) -> p (a c) q", q=128)

            def normalize(qs, gi):
                P.op("dve", lambda e: e.tensor_tensor(out=rden[:].rearrange("p (c q) -> p c q", q=128), in0=denv,
                                                      in1=esink[:, :].unsqueeze(2).to_broadcast([128, 8, 128]), op=ALU.add),
                     r=[PS(6), PS(7), "esink"], w=["rden"])
                P.op("act", lambda e: e.activation(out=rden[:], in_=rden[:], func=AF.Ln), r=["rden"], w=["rden"])
                P.op("act", lambda e: e.activation(out=rden[:], in_=rden[:], func=AF.Exp, scale=-1.0), r=["rden"], w=["rden"])
                P.op("dve", lambda e: e.tensor_tensor(out=AR[:, 8:16, qs:qs + 128], in0=numv,
                                                      in1=rden[:].rearrange("p (c q) -> p c q", q=128), op=ALU.mult),
                     r=[PS(4), PS(5), "rden"], w=[ARK(8 + c, gi) for c in range(8)])

            num4 = ps[:, 4, :].rearrange("p (c q) -> p c q", q=128)
            den4 = ps[:, 5, :].rearrange("p (c q) -> p c q", q=128)

            def att_S(qb, hk):
                qs = qb * 128
                gi = gi_of(qs)
                has_prev = not (ci == 0 and qb == 0)
                kbs = ([0] if has_prev else []) + [1]
                kgi_prev = gi_of(qs - 128) if qb > 0 else None
                sp_ = 0
                pslot = (qb * 4 + hk) % 2
                for kb in kbs:
                    kcol = qs + kb * 128
                    if kb == 1:
                        kkey = "kT:%d:%d" % (hk, gi)
                    elif qb > 0:
                        kkey = "kT:%d:%d" % (hk, kgi_prev)
                    else:
                        kkey = "kTprev"
                    for half in range(2):
                        c0_ = kb * 256
                        P.op("pe", lambda e, sp_=sp_, c0_=c0_, half=half, hk=hk, kcol=kcol, qs=qs: e.matmul(
                            ps[:, 2 * sp_ + half, c0_:c0_ + 256].rearrange("p (c q) -> p c q", q=128),
                            lhsT=kT[half * 64:(half + 1) * 64, hk, kcol:kcol + 128],
                            rhs=AR[half * 64:(half + 1) * 64, 2 * hk:2 * hk + 2, qs:qs + 128], start=True, stop=True),
                            r=[kkey, ARK(2 * hk, gi), ARK(2 * hk + 1, gi)], w=[PS(2 * sp_ + half)])
                regions = [(0, 1024)] if has_prev else [(256, 512), (768, 1024)]
                for (lo, hi) in regions:
                    P.op("act", lambda e, sp_=sp_, lo=lo, hi=hi, pslot=pslot: e.activation(
                        out=pT[pslot][:, lo:hi], in_=ps[:, 2 * sp_:2 * sp_ + 2, :].rearrange("p a n -> p (a n)")[:, lo:hi],
                        func=AF.Exp, scale=0.125), r=[PS(2 * sp_), PS(2 * sp_ + 1)], w=["pT%d" % pslot])
                    P.op("dve", lambda e, lo=lo, hi=hi, pslot=pslot, hk=hk: e.tensor_tensor(
                        out=pT[pslot][:, lo:hi], in0=pT[pslot][:, lo:hi], in1=Dp[:, hk * 1024 + lo:hk * 1024 + hi], op=ALU.mult),
                        r=["pT%d" % pslot, "Dp"], w=["pT%d" % pslot])

            def att_PV(qb, hk):
                qs = qb * 128
                gi = gi_of(qs)
                has_prev = not (ci == 0 and qb == 0)
                kbs = ([0] if has_prev else []) + [1]
                pslot = (qb * 4 + hk) % 2
                hp = hk // 2
                lc0 = 2 * hk - 4 * hp
                for half in range(2):
                    for (dst, isnum) in ((num4, True), (den4, False)):
                        for ki, kb in enumerate(kbs):
                            vt = qb + kb
                            if isnum:
                                lhs = vsb[:, vt, hk * 64:(hk + 1) * 64]
                                rk = ["v:%d" % vt, "pT%d" % pslot]
                            else:
                                lhs = ones_b[:, 0:64]
                                rk = ["ones_b", "pT%d" % pslot]
                            wk_ = [PS(4)] if isnum else [PS(5)]
                            lastk = (ki == len(kbs) - 1)
                            c0_ = half * 512 + kb * 256
                            P.op("pe", lambda e, dst=dst, half=half, lc0=lc0, lhs=lhs, pslot=pslot, c0_=c0_, ki=ki, lastk=lastk: e.matmul(
                                dst[half * 64:(half + 1) * 64, lc0:lc0 + 2, :], lhsT=lhs, rhs=pT[pslot][:, c0_:c0_ + 256],
                                start=(ki == 0), stop=lastk), r=rk, w=wk_)
                if hk % 2 == 1:
                    P.op("dve", lambda e, hp=hp: e.tensor_tensor(
                        out=rden[:, 0:512].rearrange("p (c q) -> p c q", q=128), in0=den4,
                        in1=esink[:, 4 * hp:4 * hp + 4].unsqueeze(2).to_broadcast([128, 4, 128]), op=ALU.add),
                        r=[PS(5), "esink"], w=["rden"])

                    def norm_fn(hp=hp, qs=qs, gi=gi):
                        P.op("act", lambda e: e.activation(out=rden[:, 0:512], in_=rden[:, 0:512], func=AF.Ln), r=["rden"], w=["rden"])
                        P.op("act", lambda e: e.activation(out=rden[:, 0:512], in_=rden[:, 0:512], func=AF.Exp, scale=-1.0), r=["rden"], w=["rden"])
                        P.op("dve", lambda e: e.tensor_tensor(
                            out=AR[:, 8 + 4 * hp:12 + 4 * hp, qs:qs + 128], in0=num4,
                            in1=rden[:, 0:512].rearrange("p (c q) -> p c q", q=128), op=ALU.mult),
                            r=[PS(4), "rden"], w=[ARK(8 + 4 * hp + c, gi) for c in range(4)])
                    return norm_fn
                return None

            if last:
                qs = (ntiles - 1) * 128
                gi = len(groups) - 1
                P.op("pool", lambda e: e.dma_start(out=vc, in_=cv.rearrange("s k d -> k s d")), w=GT_ALL, dma="vc")
                for hk in range(4):
                    sp_ = alloc_pair(0, 2)
                    for hh in range(4):
                        h = 4 * hk + hh
                        half, chunk = h % 2, h // 2
                        cc_ = hh // 2
                        P.op("pe", lambda e, sp_=sp_, cc_=cc_, half=half, chunk=chunk, hk=hk, qs=qs: e.matmul(
                            ps[:, 2 * sp_ + half, cc_ * 128:(cc_ + 1) * 128], lhsT=kT[half * 64:(half + 1) * 64, hk, 128 + qs:128 + qs + 128],
                            rhs=AR[half * 64:(half + 1) * 64, chunk, qs:qs + 128], start=True, stop=True),
                            r=["kT:%d:%d" % (hk, gi), ARK(chunk, gi)], w=[PS(2 * sp_ + half)])
                    P.op("act", lambda e, sp_=sp_, hk=hk: e.activation(out=pTn_t[:, hk, :].rearrange("p (a n) -> p a n", a=2),
                                                                       in_=ps[:, 2 * sp_:2 * sp_ + 2, 0:256], func=AF.Exp, scale=0.125),
                         r=[PS(2 * sp_), PS(2 * sp_ + 1)], w=["pTn%d" % hk])
                    P.op("dve", lambda e, hk=hk: e.tensor_tensor(out=pTn_t[:, hk, :], in0=pTn_t[:, hk, :], in1=Dn[:, hk * 512:(hk + 1) * 512],
                                                                 op=ALU.mult), r=["pTn%d" % hk, "Dn"], w=["pTn%d" % hk])
                kslots = [kcd[:], kcT[:], pT[0][:, :].rearrange("p (s h k) -> p s h k", s=2, h=4),
                          pT[1][:, :].rearrange("p (s h k) -> p s h k", s=2, h=4)]
                kkeys = ["kcb0", "kcb1", "pT0", "pT1"]
                for bt in range(8):
                    s0 = bt * 2
                    ksl = bt % 4
                    kcb = kslots[ksl]
                    kkey = kkeys[ksl]
                    P.op("pool", lambda e, bt=bt, kcb=kcb: e.dma_start(out=kcb.rearrange("p s h k -> p (s h k)"), in_=ckT_d[:, bt, :]),
                         w=[kkey], dma="kcs%d" % ksl)
                    sbp = alloc_pair(0, 2)
                    for sl in range(2):
                        s = s0 + sl
                        for hk in range(4):
                            for half in range(2):
                                c0 = sl * 64 + hk * 16
                                P.op("pe", lambda e, sl=sl, hk=hk, half=half, c0=c0, s=s, sbp=sbp, qs=qs, kcb=kcb: e.matmul(
                                    ps[:, 2 * sbp + half, c0:c0 + 16].rearrange("p (c q) -> p c q", q=8),
                                    lhsT=kcb[half * 64:(half + 1) * 64, sl, hk, :],
                                    rhs=AR[half * 64:(half + 1) * 64, 2 * hk:2 * hk + 2, qs + s * 8:qs + s * 8 + 8],
                                    start=True, stop=True), r=[kkey, ARK(2 * hk, gi), ARK(2 * hk + 1, gi)], w=[PS(2 * sbp + half)])
                    P.op("act", lambda e, s0=s0, sbp=sbp: e.activation(
                        out=pTc[:, s0:s0 + 2, :].rearrange("p s (h c) -> p s h c", h=2),
                        in_=ps[:, 2 * sbp:2 * sbp + 2, 0:128].rearrange("p h (s c) -> p s h c", s=2),
                        func=AF.Exp, scale=0.125), r=[PS(2 * sbp), PS(2 * sbp + 1)] + GT_ALL, w=GT_ALL)
                    P.op("dve", lambda e, s0=s0: e.tensor_tensor(out=pTc[:, s0:s0 + 2, :], in0=pTc[:, s0:s0 + 2, :],
                                                                 in1=Dc[:, :].unsqueeze(1).to_broadcast([128, 2, 128]), op=ALU.mult),
                         r=GT_ALL + ["Dc"], w=GT_ALL)
                vts = ntiles
                for hk in range(4):
                    for hh in range(4):
                        h = 4 * hk + hh
                        half, cc = hh % 2, hh // 2
                        chunk = 2 * hk + cc
                        idx = half * 8 + hk * 2 + cc
                        for (dst, isnum) in ((numv, True), (denv, False)):
                            wk_ = [PS(4 + chunk // 4)] if isnum else [PS(6 + chunk // 4)]
                            lhs = vsb[:, vts, hk * 64:(hk + 1) * 64] if isnum else ones_b[:, 0:64]
                            P.op("pe", lambda e, dst=dst, half=half, chunk=chunk, lhs=lhs, hk=hk, hh=hh: e.matmul(
                                dst[half * 64:(half + 1) * 64, chunk, :], lhsT=lhs, rhs=pTn_t[:, hk, (half * 2 + hh // 2) * 128:(half * 2 + hh // 2 + 1) * 128],
                                start=True, stop=False), r=["v:%d" % vts, "pTn%d" % hk, "ones_b"], w=wk_)
                            for s in range(16):
                                lhs2 = vc[:, s, hk * 64:(hk + 1) * 64] if isnum else ones_b[:, 0:64]
                                P.op("pe", lambda e, dst=dst, half=half, chunk=chunk, lhs2=lhs2, s=s, idx=idx: e.matmul(
                                    dst[half * 64:(half + 1) * 64, chunk, s * 8:(s + 1) * 8], lhsT=lhs2,
                                    rhs=pTc[:, s, idx * 8:(idx + 1) * 8], start=False, stop=(s == 15)),
                                    r=GT_ALL + ["ones_b"], w=wk_)
                normalize(qs, gi)

            ckpt("c%dE" % ci)
            conv_state = {}

            def conv_prologue(j):
                r_ = ring_load(wcv_t[j], 3072)
                wv_ = ring[r_][:, 0:3072].rearrange("p (k t n) -> p k t n", k=8, t=3)
                ds_ = j % 2
                for tap in range(3):
                    P.op("dve", lambda e, ds_=ds_, tap=tap, j=j: e.tensor_scalar(
                        out=dgc[ds_][:, tap, :], in0=ident_b[:], scalar1=vecA[:, 72 + tap * 8 + j:73 + tap * 8 + j], scalar2=None,
                        op0=ALU.mult), r=["ident_b", "vecA"], w=["dgc%d" % ds_])
                us = j % 2
                if ci > 0:
                    P.op("dve", lambda e, us=us, j=j: e.tensor_copy(out=uext[us][:, 0:2], in_=uhalo[:, j, :]), r=["uhalo:%d" % j], w=["uh%d" % us])
                conv_state[j] = (r_, wv_, ds_, us)

            def conv_seg1(j, gi):
                r_, wv_, ds_, us = conv_state[j]
                g0, n, kind = groups[gi]
                bC, bX = 2, 3
                for t_, b in ((1, bC), (2, bX)):
                    proj8(b, n, lambda kc, t_=t_: wv_[:, kc, t_, :], lambda kc, g0=g0, n=n: hT[:, kc, g0:g0 + n],
                          "ring%d" % r_, [HK(kc, gi) for kc in range(8)])
                cs_ = (j * 2 + gi) % 2
                P.op("act", lambda e, cs_=cs_, n=n, bC=bC: e.activation(out=C_sb[cs_][:, 0:n], in_=ps[:, bC, 0:n], func=AF.Copy),
                     r=[PS(bC)], w=["C_sb%d" % cs_])
                ukey = "u:%d:%d" % (us, gi)
                if kind == "p":
                    P.op("dve", lambda e, us=us, g0=g0, n=n, bX=bX, cs_=cs_: e.tensor_tensor(
                        out=uext[us][:, 2 + g0:2 + g0 + n], in0=ps[:, bX, 0:n], in1=C_sb[cs_][:, 0:n], op=ALU.mult),
                        r=[PS(bX), "C_sb%d" % cs_], w=[ukey])
                    if last and gi == len(groups) - 2:
                        P.op("dve", lambda e, n=n, bX=bX, cs_=cs_, j=j: e.tensor_tensor(
                            out=cst[:, j, 32:34], in0=ps[:, bX, n - 2:n], in1=C_sb[cs_][:, n - 2:n], op=ALU.mult),
                            r=[PS(bX), "C_sb%d" % cs_], w=["cst:%d" % j])
                else:
                    P.op("dve", lambda e, n=n, bX=bX, cs_=cs_, j=j: e.tensor_tensor(
                        out=ues[:, j, :, 2:10], in0=sview(ps[:, bX, 0:n]), in1=sview(C_sb[cs_][:, 0:n]), op=ALU.mult),
                        r=[PS(bX), "C_sb%d" % cs_, "ues_h"], w=[ukey])
                    P.op("dve", lambda e, n=n, bX=bX, cs_=cs_, j=j: e.tensor_tensor(
                        out=cst[:, j, 0:32].rearrange("p (s r) -> p s r", r=2), in0=sview(ps[:, bX, 0:n])[:, :, 6:8],
                        in1=sview(C_sb[cs_][:, 0:n])[:, :, 6:8], op=ALU.mult),
                        r=[PS(bX), "C_sb%d" % cs_], w=["cst:%d" % j])

            def conv_seg2(j, gi):
                r_, wv_, ds_, us = conv_state[j]
                g0, n, kind = groups[gi]
                bB, bU = 6, 7
                cs_ = (j * 2 + gi) % 2
                ukey = "u:%d:%d" % (us, gi)
                proj8(bB, n, lambda kc: wv_[:, kc, 0, :], lambda kc, g0=g0, n=n: hT[:, kc, g0:g0 + n],
                      "ring%d" % r_, [HK(kc, gi) for kc in range(8)])
                P.op("act", lambda e, cs_=cs_, n=n, bB=bB: e.activation(out=B_sb[cs_][:, 0:n], in_=ps[:, bB, 0:n], func=AF.Copy),
                     r=[PS(bB)], w=["B_sb%d" % cs_])
                for tap in range(3):
                    if kind == "p":
                        rhs = uext[us][:, g0 + tap:g0 + tap + n]
                        rk = [ukey, "uh%d" % us] + (["u:%d:%d" % (us, gi - 1)] if gi > 0 else [])
                    else:
                        rhs = ues[:, j, :, tap:tap + 8]
                        rk = [ukey, "ues_h"]
                    P.op("pe", lambda e, ds_=ds_, tap=tap, rhs=rhs, bU=bU, n=n: e.matmul(
                        ps[:, bU, 0:n], lhsT=dgc[ds_][:, tap, :], rhs=rhs, start=(tap == 0), stop=(tap == 2)),
                        r=["dgc%d" % ds_] + rk, w=[PS(bU)])
                P.op("dve", lambda e, j=j, g0=g0, n=n, bU=bU, cs_=cs_: e.tensor_tensor(
                    out=AR[:, 16 + j, g0:g0 + n], in0=ps[:, bU, 0:n], in1=B_sb[cs_][:, 0:n], op=ALU.mult),
                    r=[PS(bU), "B_sb%d" % cs_], w=[ARK(16 + j, gi)])

            def conv_epilogue(j):
                r_, wv_, ds_, us = conv_state[j]
                if not last:
                    lt = ch["ntok"]
                    P.op("dve", lambda e, us=us, j=j, lt=lt: e.tensor_copy(out=uhalo[:, j, :], in_=uext[us][:, lt:lt + 2]),
                         r=["u:%d:%d" % (us, len(groups) - 1)], w=["uhalo:%d" % j])

            att_units = [(qb, hk) for qb in range(nqb) for hk in range(4)]
            segs = []
            for j in range(8):
                for gi in range(len(groups)):
                    segs.append((1, j, gi))
                    segs.append((2, j, gi))
            na, nsg = len(att_units), len(segs)
            ai = 0
            pend = [None]

            def flush_norm():
                if pend[0] is not None:
                    pend[0]()
                    pend[0] = None

            for si, (kind_, j, gi) in enumerate(segs):
                want = ((si + 1) * na + nsg - 1) // nsg
                a = None
                if ai < min(want, na):
                    a = att_units[ai]
                    ai += 1
                    att_S(*a)
                    flush_norm()
                if kind_ == 1:
                    if gi == 0:
                        conv_prologue(j)
                    conv_seg1(j, gi)
                else:
                    conv_seg2(j, gi)
                    if gi == len(groups) - 1:
                        conv_epilogue(j)
                if a is not None:
                    flush_norm()
                    pend[0] = att_PV(*a)
            while ai < na:
                a = att_units[ai]
                ai += 1
                att_S(*a)
                flush_norm()
                pend[0] = att_PV(*a)
            flush_norm()
            if not last:
                lt = ch["ntok"] - 128
                lgi = gi_of(lt)
                P.op("dve", lambda e, lt=lt: e.tensor_copy(out=kT[:, :, 0:128], in_=kT[:, :, 128 + lt:128 + lt + 128]),
                     r=["kT:%d:%d" % (hk, lgi) for hk in range(4)] + ["kT:%d:%d" % (hk, 0) for hk in range(4)], w=["kTprev"])
                P.op("dve", lambda e, nt=ntiles: e.tensor_copy(out=vsb[:, 0, :], in_=vsb[:, nt, :]),
                     r=["v:%d" % ntiles, "v:1"], w=["v:0"])
            if last:
                bp = alloc_pair(0, 2)
                for j in range(8):
                    P.op("pe", lambda e, j=j, bp=bp: e.transpose(ps[0:34, 2 * bp + j // 4, (j % 4) * 128:(j % 4 + 1) * 128], cst[:, j, :], ident_f[:]),
                         r=["cst:%d" % j, "ident_f"], w=[PS(2 * bp), PS(2 * bp + 1)])
                s_ = state["xs"]
                state["xs"] ^= 1
                P.op("act", lambda e, bp=bp, s_=s_: e.activation(out=xs[s_][0:34, :], in_=ps[0:34, 2 * bp:2 * bp + 2, :].rearrange("p a n -> p (a n)"),
                                                                 func=AF.Copy), r=[PS(2 * bp), PS(2 * bp + 1)], w=["xs%d" % s_])
                P.op("sp", lambda e, s_=s_: e.dma_start(out=o_conv_s, in_=xs[s_][0:32, :]), r=["xs%d" % s_], dma="xs%d" % s_)
                P.op("sp", lambda e, s_=s_: e.dma_start(out=o_conv_p, in_=xs[s_][32:34, :]), r=["xs%d" % s_], dma="xs%d" % s_)

            ckpt("c%dF" % ci)
            for j in range(8):
                if ci == 0:
                    ada_unit(1 + j // 4, j % 4)
                    if j == 7:
                        mod_scale(32, 56)
                r_ = ring_load(wmg_t[j], 4096)
                wv_ = ring[r_][:, :].rearrange("p (k t n) -> p k t n", k=8, t=4)
                for gi, (g0, n, kind) in enumerate(groups):
                    bA, bBb, bYa, bYb = alloc(), alloc(), alloc(), alloc()
                    hk_ = [HK(kc, gi) for kc in range(8)]
                    proj8(bA, n, lambda kc: wv_[:, kc, 0, :], lambda kc, g0=g0, n=n: hT[:, kc, g0:g0 + n], "ring%d" % r_, hk_)
                    proj8(bBb, n, lambda kc: wv_[:, kc, 1, :], lambda kc, g0=g0, n=n: hT[:, kc, g0:g0 + n], "ring%d" % r_, hk_)
                    proj8(bYa, n, lambda kc: wv_[:, kc, 2, :], lambda kc, g0=g0, n=n: AR[:, 16 + kc, g0:g0 + n], "ring%d" % r_,
                          [ARK(16 + kc, gi) for kc in range(8)])
                    proj8(bYb, n, lambda kc: wv_[:, kc, 3, :], lambda kc, g0=g0, n=n: AR[:, 8 + kc, g0:g0 + n], "ring%d" % r_,
                          [ARK(8 + kc, gi) for kc in range(8)])
                    ss_ = (j * 2 + gi) % 2
                    P.op("act", lambda e, ss_=ss_, n=n, bA=bA: e.activation(out=sga[ss_][:, 0:n], in_=ps[:, bA, 0:n], func=AF.Sigmoid),
                         r=[PS(bA)], w=["sga%d" % ss_])
                    P.op("act", lambda e, ss_=ss_, n=n, bBb=bBb: e.activation(out=sgb[ss_][:, 0:n], in_=ps[:, bBb, 0:n], func=AF.Sigmoid),
                         r=[PS(bBb)], w=["sgb%d" % ss_])
                    P.op("dve", lambda e, ss_=ss_, n=n, bYa=bYa: e.tensor_tensor(out=t1s[ss_][:, 0:n], in0=ps[:, bYa, 0:n], in1=sga[ss_][:, 0:n], op=ALU.mult),
                         r=[PS(bYa), "sga%d" % ss_], w=["t1_%d" % ss_])
                    P.op("dve", lambda e, ss_=ss_, n=n, bYb=bYb: e.tensor_tensor(out=t2s[ss_][:, 0:n], in0=ps[:, bYb, 0:n], in1=sgb[ss_][:, 0:n], op=ALU.mult),
                         r=[PS(bYb), "sgb%d" % ss_], w=["t2_%d" % ss_])
                    P.op("dve", lambda e, ss_=ss_, j=j, g0=g0, n=n: e.tensor_tensor(out=AR[:, j, g0:g0 + n], in0=t1s[ss_][:, 0:n], in1=t2s[ss_][:, 0:n], op=ALU.add),
                         r=["t1_%d" % ss_, "t2_%d" % ss_], w=[ARK(j, gi)])

            ckpt("c%dG" % ci)
            wmix_v = []
            for u in range(2):
                r_ = ring_load(wmix_t[u], 4096)
                wmix_v.append((r_, ring[r_][:, :].rearrange("p (c k n) -> p c k n", c=4, k=8)))
            for gi, (g0, n, kind) in enumerate(groups):
                for j in range(8):
                    r_, wv_ = wmix_v[j // 4]
                    jj = j % 4
                    b = alloc()
                    proj8(b, n, lambda kc, jj=jj, wv_=wv_: wv_[:, jj, kc, :], lambda kc, g0=g0, n=n: AR[:, kc, g0:g0 + n],
                          "ring%d" % r_, [ARK(kc, gi) for kc in range(8)])
                    resid_add(b, n, j, g0, gi, kind, 16)
                    if gi > 0 and j == 3:
                        norm_B(ch, gi - 1, 32, 24)
                norm_A(ch, gi)
            ckpt("c%dH" % ci)
            norm_B(ch, len(groups) - 1, 32, 24)

            ckpt("c%dI" % ci)
            ffn_state = {}

            def ffn_s1(jf, gi, gs_):
                g0, n, kind = groups[gi]
                us = jf % 3
                if gi == 0:
                    r_ = ring_load(wup_t[jf], 2048)
                    wv_ = ring[r_][:, 0:2048].rearrange("p (k t n) -> p k t n", k=8, t=2)
                    ffn_state[jf] = (r_, wv_)
                    if ci > 0:
                        P.op("dve", lambda e, us=us, jf=jf: e.tensor_copy(out=uext[us][:, 0:2], in_=ahalo[:, jf, :]), r=["ahalo:%d" % jf], w=["uh%d" % us])
                    else:
                        P.op("dve", lambda e, us=us: e.memset(uext[us][:, 0:2], 0.0), w=["uh%d" % us])
                r_, wv_ = ffn_state[jf]
                bA, bV = alloc(), alloc()
                hk_ = [HK(kc, gi) for kc in range(8)]
                proj8(bA, n, lambda kc: wv_[:, kc, 0, :], lambda kc, g0=g0, n=n: hT[:, kc, g0:g0 + n], "ring%d" % r_, hk_)
                proj8(bV, n, lambda kc: wv_[:, kc, 1, :], lambda kc, g0=g0, n=n: hT[:, kc, g0:g0 + n], "ring%d" % r_, hk_)
                ukey = "u:%d:%d" % (us, gi)
                P.op("act", lambda e, gs_=gs_, n=n, bA=bA, jf=jf: e.activation(
                    out=C_sb[gs_][:, 0:n], in_=ps[:, bA, 0:n], func=AF.Identity, scale=vecB[:, 2 * NJF + jf:2 * NJF + jf + 1]),
                    r=[PS(bA), "vecB"], w=["C_sb%d" % gs_])
                if kind == "p":
                    P.op("act", lambda e, us=us, g0=g0, n=n, bA=bA: e.activation(out=uext[us][:, 2 + g0:2 + g0 + n], in_=ps[:, bA, 0:n], func=AF.Copy),
                         r=[PS(bA)], w=[ukey])
                    if last and gi == len(groups) - 2:
                        P.op("act", lambda e, n=n, bA=bA, jf=jf: e.activation(out=fst[:, jf, 32:34], in_=ps[:, bA, n - 2:n], func=AF.Copy),
                             r=[PS(bA)], w=["fst:%d" % jf])
                else:
                    P.op("act", lambda e, n=n, bA=bA, jf=jf: e.activation(out=aes[:, jf, :, 2:10], in_=sview(ps[:, bA, 0:n]), func=AF.Copy),
                         r=[PS(bA), "aes_h"], w=[ukey])
                    P.op("act", lambda e, n=n, bA=bA, jf=jf: e.activation(
                        out=fst[:, jf, 0:32].rearrange("p (s r) -> p s r", r=2), in_=sview(ps[:, bA, 0:n])[:, :, 6:8], func=AF.Copy),
                        r=[PS(bA)], w=["fst:%d" % jf])
                if gi == len(groups) - 1 and not last:
                    lt = ch["ntok"]
                    P.op("dve", lambda e, us=us, jf=jf, lt=lt: e.tensor_copy(out=ahalo[:, jf, :], in_=uext[us][:, lt:lt + 2]),
                         r=[ukey], w=["ahalo:%d" % jf])
                return bV

            def ffn_s2(jf, gi, gs_):
                g0, n, kind = groups[gi]
                us = jf % 3
                ukey = "u:%d:%d" % (us, gi)
                if kind == "p":
                    taps = [uext[us][:, g0 + tap:g0 + tap + n] for tap in range(3)]
                    accv, outv = C_sb[gs_][:, 0:n], B_sb[gs_][:, 0:n]
                    rk = [ukey, "uh%d" % us] + (["u:%d:%d" % (us, gi - 1)] if gi > 0 else [])
                else:
                    taps = [aes[:, jf, :, tap:tap + 8] for tap in range(3)]
                    accv, outv = sview(C_sb[gs_][:, 0:n]), sview(B_sb[gs_][:, 0:n])
                    rk = [ukey, "aes_h"]
                wcol = [vecB[:, tap * NJF + jf:tap * NJF + jf + 1] for tap in range(3)]
                P.op("dve", lambda e, accv=accv, taps=taps, wcol=wcol: e.scalar_tensor_tensor(
                    out=accv, in0=taps[1], scalar=wcol[1], in1=accv, op0=ALU.mult, op1=ALU.add),
                    r=rk + ["vecB", "C_sb%d" % gs_], w=["C_sb%d" % gs_])
                P.op("dve", lambda e, accv=accv, outv=outv, taps=taps, wcol=wcol: e.scalar_tensor_tensor(
                    out=outv, in0=taps[0], scalar=wcol[0], in1=accv, op0=ALU.mult, op1=ALU.add),
                    r=rk + ["vecB", "C_sb%d" % gs_], w=["B_sb%d" % gs_])
                P.op("act", lambda e, gs_=gs_, n=n: e.activation(out=ge[gs_][:, 0:n], in_=B_sb[gs_][:, 0:n], func=AF.Gelu),
                     r=["B_sb%d" % gs_], w=["ge%d" % gs_])

            def ffn_s3(jf, gi, gs_, bV):
                g0, n, kind = groups[gi]
                P.op("dve", lambda e, gs_=gs_, jf=jf, g0=g0, n=n, bV=bV: e.tensor_tensor(
                    out=AR[:, jf, g0:g0 + n], in0=ps[:, bV, 0:n], in1=ge[gs_][:, 0:n], op=ALU.mult),
                    r=[PS(bV), "ge%d" % gs_], w=[ARK(jf, gi)])

            ng = len(groups)
            units = [(jf, 0) for jf in range(3)] + [(jf, gi) for jf in range(3) for gi in range(1, ng)] + \
                    [(jf, gi) for jf in range(3, NJF) for gi in range(ng)]
            bvs = {}
            for t in range(len(units) + 2):
                if t < len(units):
                    bvs[t] = ffn_s1(*units[t], t % 2)
                if 0 <= t - 1 < len(units):
                    ffn_s2(*units[t - 1], (t - 1) % 2)
                if 0 <= t - 2 < len(units):
                    ffn_s3(*units[t - 2], (t - 2) % 2, bvs[t - 2])
            if last:
                for rnd in range(3):
                    j0 = rnd * 8
                    nj = min(8, NJF - j0)
                    bp = alloc_pair(0, 4)
                    for jj in range(nj):
                        P.op("pe", lambda e, jj=jj, j0=j0, bp=bp: e.transpose(
                            ps[0:34, 2 * bp + jj // 4, (jj % 4) * 128:(jj % 4 + 1) * 128], fst[:, j0 + jj, :], ident_f[:]),
                            r=["fst:%d" % (j0 + jj), "ident_f"], w=[PS(2 * bp), PS(2 * bp + 1)])
                    s_ = state["xs"]
                    state["xs"] ^= 1
                    P.op("act", lambda e, bp=bp, s_=s_, nj=nj: e.activation(
                        out=xs[s_][0:34, 0:nj * 128], in_=ps[0:34, 2 * bp:2 * bp + 2, :].rearrange("p a n -> p (a n)")[:, 0:nj * 128],
                        func=AF.Copy), r=[PS(2 * bp), PS(2 * bp + 1)], w=["xs%d" % s_])
                    P.op("sp", lambda e, s_=s_, j0=j0, nj=nj: e.dma_start(out=o_ffn_s[:, j0 * 128:(j0 + nj) * 128], in_=xs[s_][0:32, 0:nj * 128]),
                         r=["xs%d" % s_], dma="xs%d" % s_)
                    P.op("sp", lambda e, s_=s_, j0=j0, nj=nj: e.dma_start(out=o_ffn_p[:, j0 * 128:(j0 + nj) * 128], in_=xs[s_][32:34, 0:nj * 128]),
                         r=["xs%d" % s_], dma="xs%d" % s_)

            ckpt("c%dJ" % ci)
            def final_pieces(gi):
                g0, n, kind = groups[gi]
                base_ = ch["base"]
                xk_ = ["xT:%d:%d" % (kc, gi) for kc in range(8)]
                stb = {}
                out = []

                def sq(kc):
                    sl_ = kc % 2
                    P.op("act", lambda e, kc=kc, sl_=sl_: e.activation(out=pT[sl_][:, 0:n], in_=xT[:, kc, g0:g0 + n], func=AF.Square),
                         r=[xk_[kc]], w=["pT%d" % sl_])

                def p1(kc):
                    if kc == 0:
                        stb["b"] = alloc()
                        sq(0)
                        sq(1)
                    b = stb["b"]
                    sl_ = kc % 2
                    P.op("pe", lambda e, kc=kc, b=b, sl_=sl_: e.matmul(ps[:, b, 0:n], lhsT=ones_b[:], rhs=pT[sl_][:, 0:n],
                                                                      start=(kc == 0), stop=(kc == 7)),
                         r=["ones_b", "pT%d" % sl_], w=[PS(b)])
                    if kc + 2 < 8:
                        sq(kc + 2)

                def p2():
                    b = stb["b"]
                    P.op("act", lambda e, b=b: e.activation(out=rs[:, 0:n], in_=ps[:, b, 0:n], func=AF.Ln, scale=1.0 / D, bias=1e-6),
                         r=[PS(b)], w=["rs"])
                    P.op("act", lambda e: e.activation(out=rstd[:, 0:n], in_=rs[:, 0:n], func=AF.Exp, scale=-0.5), r=["rs"], w=["rstd"])

                def p3(kc):
                    P.op("dve", lambda e, kc=kc: e.scalar_tensor_tensor(
                        out=xT[:, kc, g0:g0 + n], in0=xT[:, kc, g0:g0 + n], scalar=vecA[:, 64 + kc:65 + kc], in1=rstd[:, 0:n],
                        op0=ALU.mult, op1=ALU.mult), r=[xk_[kc], "vecA", "rstd"], w=[xk_[kc]])

                def p4(tl):
                    tok = g0 + tl * 128
                    row0 = TP if kind == "s" else base_ + tok
                    bp = alloc_pair(0, 4)
                    for kc in range(8):
                        P.op("pe", lambda e, kc=kc, bp=bp, tok=tok: e.transpose(
                            ps[:, 2 * bp + kc // 4, (kc % 4) * 128:(kc % 4 + 1) * 128], xT[:, kc, tok:tok + 128], ident_f[:]),
                            r=[xk_[kc], "ident_f"], w=[PS(2 * bp), PS(2 * bp + 1)])
                    s_ = state["xs"]
                    state["xs"] ^= 1
                    P.op("act", lambda e, bp=bp, s_=s_: e.activation(out=xs[s_][:], in_=ps[:, 2 * bp:2 * bp + 2, :].rearrange("p a n -> p (a n)"),
                                                                     func=AF.Copy), r=[PS(2 * bp), PS(2 * bp + 1)], w=["xs%d" % s_])
                    P.op("sp", lambda e, s_=s_, row0=row0: e.dma_start(out=y_d[row0:row0 + 128, :], in_=xs[s_][:]),
                         r=["xs%d" % s_], dma="xs%d" % s_)

                for kc in range(8):
                    out.append(lambda kc=kc: p1(kc))
                out.append(p2)
                for kc in range(8):
                    out.append(lambda kc=kc: p3(kc))
                for tl in range(n // 128):
                    out.append(lambda tl=tl: p4(tl))
                return out

            def final_A(gi):
                pass

            def final_B(gi):
                g0, n, kind = groups[gi]
                xk_ = ["xT:%d:%d" % (kc, gi) for kc in range(8)]
                b = alloc()
                for kc in range(8):
                    sl_ = kc % 2
                    P.op("act", lambda e, kc=kc, g0=g0, n=n, sl_=sl_: e.activation(out=pT[sl_][:, 0:n], in_=xT[:, kc, g0:g0 + n], func=AF.Square),
                         r=[xk_[kc]], w=["pT%d" % sl_])
                    P.op("pe", lambda e, kc=kc, n=n, b=b, sl_=sl_: e.matmul(ps[:, b, 0:n], lhsT=ones_b[:], rhs=pT[sl_][:, 0:n],
                                                                          start=(kc == 0), stop=(kc == 7)),
                         r=["ones_b", "pT%d" % sl_], w=[PS(b)])
                P.op("act", lambda e, n=n, b=b: e.activation(out=rs[:, 0:n], in_=ps[:, b, 0:n], func=AF.Ln, scale=1.0 / D, bias=1e-6),
                     r=[PS(b)], w=["rs"])
                P.op("act", lambda e, n=n: e.activation(out=rstd[:, 0:n], in_=rs[:, 0:n], func=AF.Exp, scale=-0.5), r=["rs"], w=["rstd"])
                for kc in range(8):
                    P.op("dve", lambda e, kc=kc, g0=g0, n=n: e.scalar_tensor_tensor(
                        out=xT[:, kc, g0:g0 + n], in0=xT[:, kc, g0:g0 + n], scalar=vecA[:, 64 + kc:65 + kc], in1=rstd[:, 0:n],
                        op0=ALU.mult, op1=ALU.mult), r=[xk_[kc], "vecA", "rstd"], w=[xk_[kc]])

            def final_C(gi):
                g0, n, kind = groups[gi]
                xk_ = ["xT:%d:%d" % (kc, gi) for kc in range(8)]
                for tl in range(n // 128):
                    tok = g0 + tl * 128
                    row0 = TP if kind == "s" else ch["base"] + tok
                    bp = alloc_pair(0, 4)
                    for kc in range(8):
                        P.op("pe", lambda e, kc=kc, bp=bp, tok=tok: e.transpose(
                            ps[:, 2 * bp + kc // 4, (kc % 4) * 128:(kc % 4 + 1) * 128], xT[:, kc, tok:tok + 128], ident_f[:]),
                            r=[xk_[kc], "ident_f"], w=[PS(2 * bp), PS(2 * bp + 1)])
                    s_ = state["xs"]
                    state["xs"] ^= 1
                    P.op("act", lambda e, bp=bp, s_=s_: e.activation(out=xs[s_][:], in_=ps[:, 2 * bp:2 * bp + 2, :].rearrange("p a n -> p (a n)"),
                                                                     func=AF.Copy), r=[PS(2 * bp), PS(2 * bp + 1)], w=["xs%d" % s_])
                    P.op("sp", lambda e, s_=s_, row0=row0: e.dma_start(out=y_d[row0:row0 + 128, :], in_=xs[s_][:]),
                         r=["xs%d" % s_], dma="xs%d" % s_)

            npre = 0
            if not last:
                nxt = CHUNKS[ci + 1]
                npre = nxt["ntok"] // 128 - (1 if ci + 1 == len(CHUNKS) - 1 else 0)
            for j in range(8):
                if 1 <= j <= npre:
                    pre_norm(ci + 1, j - 1, (j - 1) % 2)
                r_ = ring_load(wdn_t[j], 2816)
                wv_ = ring[r_][:, 0:2816].rearrange("p (k n) -> p k n", k=NJF)
                for gi, (g0, n, kind) in enumerate(groups):
                    b = alloc()
                    for kc in range(NJF):
                        ld_ = wv_[:, kc, :]
                        P.op("pe", lambda e, kc=kc, g0=g0, n=n, b=b, ld_=ld_: e.matmul(ps[:, b, 0:n], lhsT=ld_, rhs=AR[:, kc, g0:g0 + n],
                                                                             start=(kc == 0), stop=(kc == NJF - 1)),
                             r=["ring%d" % r_, ARK(kc, gi)], w=[PS(b)])
                    resid_add(b, n, j, g0, gi, kind, 40)
            ckpt("c%dK" % ci)
            pieces = []
            for gi in range(len(groups)):
                pieces.extend(final_pieces(gi))
            if last:
                for p_ in pieces:
                    p_()
            else:
                nci = ci + 1
                nch = CHUNKS[nci]
                nnt = nch["ntok"] // 128
                nlast = (nci == len(CHUNKS) - 1)
                tiles = ([nnt - 1] if nlast else []) + list(range(nnt - 1 if nlast else nnt))
                prefetch, bsteps = make_B_steps(nci, tiles)
                prefetch()
                fillers.extend(pieces)
                if nlast:
                    fillers.append(bsteps[0])
                    fillers.append(lambda nch=nch: (norm_A(nch, len(nch["groups"]) - 1), norm_B(nch, len(nch["groups"]) - 1, 8, 0)))
                    fillers.extend(bsteps[1:])
                else:
                    fillers.extend(bsteps)

            ckpt("c%dL" % ci)

        P.emit(st)
    return nc


_NC_CACHE = {}


def _prep_shared(inp):
    f = np.float32
    w_in = np.asarray(inp["w_in"][0], f)
    W = w_in.reshape(8, 128, 6656)
    sh = {}
    sh["wada_t"] = np.ascontiguousarray(np.asarray(inp["w_ada"][0], f).reshape(8, 128, 12, 4, 128).transpose(2, 1, 3, 0, 4)).reshape(12, 128, 4096)
    sh["wq_t"] = np.ascontiguousarray(W[:, :, 3072:4096].reshape(8, 128, 2, 4, 128).transpose(2, 1, 3, 0, 4)).reshape(2, 128, 4096)
    wk = W[:, :, 4096:4352].reshape(8, 128, 4, 64).transpose(1, 2, 0, 3)
    sh["wk_t"] = np.ascontiguousarray(np.stack([wk, wk], axis=3)).reshape(128, 4096)
    sh["wv_t"] = np.ascontiguousarray(W[:, :, 4352:4608].transpose(1, 0, 2)).reshape(128, 2048)
    sh["wcv_t"] = np.ascontiguousarray(W[:, :, 0:3072].reshape(8, 128, 3, 8, 128).transpose(3, 1, 0, 2, 4)).reshape(8, 128, 3072)
    comps = np.stack([W[:, :, 4608:5632], W[:, :, 5632:6656],
                      np.asarray(inp["w_conv_out"][0], f).reshape(8, 128, 1024),
                      np.asarray(inp["w_attn_out"][0], f).reshape(8, 128, 1024)], axis=0)
    sh["wmg_t"] = np.ascontiguousarray(comps.reshape(4, 8, 128, 8, 128).transpose(3, 2, 1, 0, 4)).reshape(8, 128, 4096)
    sh["wmix_t"] = np.ascontiguousarray(np.asarray(inp["w_mix_out"][0], f).reshape(8, 128, 2, 4, 128).transpose(2, 1, 3, 0, 4)).reshape(2, 128, 4096)
    sh["wup_t"] = np.ascontiguousarray(np.asarray(inp["w_up"][0], f).reshape(8, 128, 2, NJF, 128).transpose(3, 1, 0, 2, 4)).reshape(NJF, 128, 2048)
    sh["wdn_t"] = np.ascontiguousarray(np.asarray(inp["w_down"][0], f).reshape(NJF, 128, 8, 128).transpose(2, 1, 0, 3)).reshape(8, 128, 2816)
    rowsA = np.zeros((128, 128), f)
    rowsA[0:48] = np.asarray(inp["b_ada"][0], f).reshape(48, 128)
    rowsA[48:56] = np.asarray(inp["norm1_g"][0], f).reshape(8, 128)
    rowsA[56:64] = np.asarray(inp["norm2_g"][0], f).reshape(8, 128)
    rowsA[64:72] = np.asarray(inp["final_g"], f).reshape(8, 128)
    rowsA[72:96] = np.asarray(inp["conv_w"][0], f).reshape(24, 128)
    rowsB = np.zeros((128, 128), f)
    rowsB[0:66] = np.asarray(inp["ffn_conv_w"][0], f).reshape(66, 128)
    sh["rowsA"], sh["rowsB"] = rowsA, rowsB
    sinks = np.asarray(inp["attn_sinks"][0], f)
    p = np.arange(128)[:, None]
    c = np.arange(8)[None, :]
    sh["sinkT"] = np.ascontiguousarray(sinks[2 * c + p // 64]).astype(f)
    sh["ident"] = np.eye(128, dtype=f)
    slopes = np.exp2(-8.0 * np.arange(1, 17, dtype=np.float64) / 16.0)
    k = np.arange(128)[:, None]
    q = np.arange(128)[None, :]
    Dp = np.zeros((128, 4, 2, 2, 2, 128), np.float64)
    for hk in range(4):
        for hh in range(4):
            sl = slopes[4 * hk + hh]
            half, cc = hh % 2, hh // 2
            dist0 = 128 + q - k
            Dp[:, hk, half, 0, cc, :] = np.where(dist0 <= 128, np.exp(-sl * dist0), 0.0)
            dist1 = q - k
            Dp[:, hk, half, 1, cc, :] = np.where(dist1 >= 0, np.exp(-sl * np.maximum(dist1, 0)), 0.0)
    sh["Dp"] = Dp.reshape(128, 4096).astype(f)
    ks, kt = k // 8, k % 8
    qs_, qt = q // 8, q % 8
    Dn = np.zeros((128, 4, 2, 2, 128), np.float64)
    for hk in range(4):
        for hh in range(4):
            sl = slopes[4 * hk + hh]
            Dn[:, hk, hh % 2, hh // 2, :] = np.where((ks == qs_) & (kt <= qt), np.exp(-sl * np.maximum(qt - kt, 0)), 0.0)
    sh["Dn"] = Dn.reshape(128, 2048).astype(f)
    Dc = np.zeros((128, 16, 8), np.float64)
    jj = np.arange(128)[:, None]
    ii = np.arange(8)[None, :]
    for hk in range(4):
        for half in range(2):
            for cc in range(2):
                sl = slopes[4 * hk + 2 * cc + half]
                dist = 128 + ii - jj
                Dc[:, half * 8 + hk * 2 + cc, :] = np.where(jj >= ii, np.exp(-sl * dist), 0.0)
    sh["Dc"] = Dc.reshape(128, 128).astype(f)
    return sh


def kernel(**inp):
    f = np.float32
    if "nc" not in _NC_CACHE:
        _NC_CACHE["nc"] = build_nc()
    nc = _NC_CACHE["nc"]
    sh = _prep_shared(inp)
    xp = np.asarray(inp["x_prompt"], f)
    xsm = np.asarray(inp["x_sample"], f)
    cp = np.asarray(inp["c_prompt"], f)
    csm = np.asarray(inp["c_sample"], f)
    stc = np.asarray(inp["state_conv"][0], f)
    stf = np.asarray(inp["state_ffn_conv"][0], f)
    ckw = np.asarray(inp["cache_k_win"][0], f)
    cvw = np.asarray(inp["cache_v_win"][0], f)
    in_maps = []
    for c in range(NCORES):
        m = dict(sh)
        m["xin"] = np.concatenate([xp[c], xsm[c * 16:(c + 1) * 16].reshape(128, D)], axis=0)
        cv_ = np.zeros((32, D), f)
        cv_[0] = cp[c]
        cv_[1:17] = csm[c * 16:(c + 1) * 16]
        m["cvec"] = cv_
        m["st_conv"] = np.ascontiguousarray(stc[c * 16:(c + 1) * 16].reshape(32, D))
        m["st_ffn"] = np.ascontiguousarray(stf[c * 16:(c + 1) * 16].reshape(32, DFF))
        m["ck"] = np.ascontiguousarray(ckw[c * 16:(c + 1) * 16].reshape(16, 128, 256))
        m["cv"] = np.ascontiguousarray(cvw[c * 16:(c + 1) * 16].reshape(16, 128, 256))
        kt = ckw[c * 16:(c + 1) * 16].reshape(8, 2, 128, 4, 64).transpose(4, 0, 1, 3, 2)
        m["ckT"] = np.ascontiguousarray(np.concatenate([kt, kt], axis=0)).reshape(128, 8, 1024)
        in_maps.append(m)
    res = run_bass_kernel_spmd(nc, in_maps, core_ids=list(range(NCORES)))
    R = res.results
    y_p = np.stack([R[c]["y"][0:TP] for c in range(NCORES)], 0)
    y_s = np.concatenate([R[c]["y"][TP:].reshape(16, 8, D) for c in range(NCORES)], 0)
    conv_p = np.stack([R[c]["o_conv_p"] for c in range(NCORES)], 0)[None]
    k_p = np.stack([R[c]["o_k_p"].reshape(128, 4, 64) for c in range(NCORES)], 0)[None]
    v_p = np.stack([R[c]["o_v_p"].reshape(128, 4, 64) for c in range(NCORES)], 0)[None]
    ffn_p = np.stack([R[c]["o_ffn_p"] for c in range(NCORES)], 0)[None]
    conv_s = np.concatenate([R[c]["o_conv_s"].reshape(16, 2, D) for c in range(NCORES)], 0)[None]
    k_s = np.concatenate([R[c]["o_k_s"].reshape(16, 128, 4, 64) for c in range(NCORES)], 0)[None]
    v_s = np.concatenate([R[c]["o_v_s"].reshape(16, 128, 4, 64) for c in range(NCORES)], 0)[None]
    ffn_s = np.concatenate([R[c]["o_ffn_s"].reshape(16, 2, DFF) for c in range(NCORES)], 0)[None]
    outs = (y_p, y_s, conv_p, k_p, v_p, ffn_p, conv_s, k_s, v_s, ffn_s)
    return tuple(np.ascontiguousarray(o, dtype=f) for o in outs)
```
